# Optimizing a Trainium2 kernel written in Bass

```python
import math
import jax, jax.numpy as jnp
from jax import lax
import numpy as np

D_MODEL = 2048
BATCH = 8
SEQ = 2048
DEPTH = 4

N_META = 16
D_MIX = D_MODEL
D_S5 = D_MIX // 2
S5_GROUP = 16
S5_GROUPS = D_S5 // S5_GROUP
S5_STATE = 64
D_ML = D_MIX - D_S5
ML_HEADS = 8
ML_DV = D_ML // ML_HEADS
ML_DK = ML_DV // 2
ML_CHUNK = 64
META_PAD = (-N_META) % ML_CHUNK
QK_CONV = 3
C_QK = 2 * ML_HEADS * ML_DK
C_GATES = 4 * ML_HEADS
D_IN = D_S5 + C_QK + D_ML + D_ML + C_GATES
D_FF = 5632
FFN_CONV = 3
EPS = 1e-6
NEG = -1e30

kernel_name = 'hybrid_s5_mlstm_encoder'


def rmsnorm(x, g):
    xf = x.astype(jnp.float32)
    y = xf * lax.rsqrt(jnp.mean(xf * xf, axis=-1, keepdims=True) + EPS)
    return (y * g.astype(jnp.float32)).astype(x.dtype)


def dwconv_centred(x, w, b):
    K = w.shape[0]
    r = K // 2
    L = x.shape[1]
    xp = jnp.pad(x, ((0, 0), (r, r), (0, 0)))
    out = b
    for j in range(K):
        out = out + xp[:, j:j + L, :] * w[j]
    return out


def _ssm_combine(e1, e2):
    a1r, a1i, b1r, b1i = e1
    a2r, a2i, b2r, b2i = e2
    return (a2r * a1r - a2i * a1i,
            a2r * a1i + a2i * a1r,
            a2r * b1r - a2i * b1i + b2r,
            a2r * b1i + a2i * b1r + b2i)


def s5_direction(u, a_re, a_im, log_dt, b_re, b_im, c_re, c_im, reverse):
    f32 = jnp.float32
    a_re = jnp.minimum(a_re.astype(f32), -1e-4)
    a_im = a_im.astype(f32)
    dt = jnp.exp(log_dt.astype(f32))[:, None]
    mag = jnp.exp(a_re * dt)
    lam_r = mag * jnp.cos(a_im * dt)
    lam_i = mag * jnp.sin(a_im * dt)
    den = a_re * a_re + a_im * a_im
    p = lam_r - 1.0
    z_r = (p * a_re + lam_i * a_im) / den
    z_i = (lam_i * a_re - p * a_im) / den
    b_re = b_re.astype(f32)
    b_im = b_im.astype(f32)
    bb_r = z_r[..., None] * b_re - z_i[..., None] * b_im
    bb_i = z_r[..., None] * b_im + z_i[..., None] * b_re
    x_r = jnp.einsum('blgh,gph->blgp', u, bb_r)
    x_i = jnp.einsum('blgh,gph->blgp', u, bb_i)
    L = u.shape[1]
    lam_r_seq = jnp.broadcast_to(lam_r, (1, L) + lam_r.shape)
    lam_i_seq = jnp.broadcast_to(lam_i, (1, L) + lam_i.shape)
    _, _, s_r, s_i = lax.associative_scan(
        _ssm_combine, (lam_r_seq, lam_i_seq, x_r, x_i), reverse=reverse, axis=1)
    return (jnp.einsum('blgp,ghp->blgh', s_r, c_re.astype(f32))
            - jnp.einsum('blgp,ghp->blgh', s_i, c_im.astype(f32)))


def s5_mixer(u, a_re, a_im, log_dt, b_re, b_im, c_re, c_im, d_skip, w_glu, b_glu):
    Bsz, L, _ = u.shape
    uf = u.astype(jnp.float32).reshape(Bsz, L, S5_GROUPS, S5_GROUP)
    y = (s5_direction(uf, a_re[0], a_im[0], log_dt[0], b_re[0], b_im[0], c_re[0], c_im[0], False)
         + s5_direction(uf, a_re[1], a_im[1], log_dt[1], b_re[1], b_im[1], c_re[1], c_im[1], True)
         + d_skip.astype(jnp.float32) * uf)
    y = jax.nn.gelu(y.reshape(Bsz, L, D_S5))
    return y * jax.nn.sigmoid(y @ w_glu.astype(jnp.float32) + b_glu.astype(jnp.float32))


def mlstm_chunkwise(q, k, v, log_i, log_f):
    Bsz, H, T, dk = q.shape
    dv = v.shape[-1]
    c = ML_CHUNK
    N = T // c
    q = q.reshape(Bsz, H, N, c, dk)
    k = k.reshape(Bsz, H, N, c, dk)
    v = v.reshape(Bsz, H, N, c, dv)
    li = log_i.reshape(Bsz, H, N, c)
    lf = log_f.reshape(Bsz, H, N, c)
    b = jnp.cumsum(lf, axis=-1)
    g = b[..., -1]
    a = g[..., None] - b + li
    m_loc = jnp.max(a, axis=-1)
    w_loc = jnp.exp(a - m_loc[..., None])
    kv_loc = jnp.einsum('bhnc,bhnck,bhncv->bhnkv', w_loc, k, v)
    n_loc = jnp.einsum('bhnc,bhnck->bhnk', w_loc, k)

    def step(carry, inp):
        C, nv, m = carry
        g_n, m_l, kv_l, n_l = inp
        m_new = jnp.maximum(g_n + m, m_l)
        s_old = jnp.exp(g_n + m - m_new)
        s_loc = jnp.exp(m_l - m_new)
        C_new = s_old[..., None, None] * C + s_loc[..., None, None] * kv_l
        n_new = s_old[..., None] * nv + s_loc[..., None] * n_l
        return (C_new, n_new, m_new), (C, nv, m)

    init = (jnp.zeros((Bsz, H, dk, dv), jnp.float32),
            jnp.zeros((Bsz, H, dk), jnp.float32),
            jnp.full((Bsz, H), NEG, jnp.float32))
    xs = (jnp.moveaxis(g, 2, 0), jnp.moveaxis(m_loc, 2, 0),
          jnp.moveaxis(kv_loc, 2, 0), jnp.moveaxis(n_loc, 2, 0))
    _, (C_prev, n_prev, m_prev) = lax.scan(step, init, xs)
    C_prev = jnp.moveaxis(C_prev, 0, 2)
    n_prev = jnp.moveaxis(n_prev, 0, 2)
    m_prev = jnp.moveaxis(m_prev, 0, 2)

    D = b[..., :, None] - b[..., None, :] + li[..., None, :]
    mask = jnp.tril(jnp.ones((c, c), dtype=bool))
    Dm = jnp.where(mask, D, NEG)
    m_inter = b + m_prev[..., None]
    m_t = jnp.maximum(m_inter, jnp.max(Dm, axis=-1))
    w = jnp.where(mask, jnp.exp(Dm - m_t[..., None]), 0.0)
    s = jnp.einsum('bhntk,bhnsk->bhnts', q, k) * w
    s_inter = jnp.exp(m_inter - m_t)
    num = (jnp.einsum('bhnts,bhnsv->bhntv', s, v)
           + s_inter[..., None] * jnp.einsum('bhntk,bhnkv->bhntv', q, C_prev))
    den = jnp.sum(s, axis=-1) + s_inter * jnp.einsum('bhntk,bhnk->bhnt', q, n_prev)
    h = num / jnp.maximum(jnp.abs(den), jnp.exp(-m_t))[..., None]
    return h.reshape(Bsz, H, T, dv)


def _pad_front(x, n, value):
    widths = [(0, 0)] * x.ndim
    widths[2] = (n, 0)
    return jnp.pad(x, widths, constant_values=value)


def mlstm_mixer(qk_in, v_in, o_pre, gates_pre, conv_w, conv_b, gate_b, head_g):
    f32 = jnp.float32
    Bsz, L, _ = qk_in.shape
    qk = jax.nn.silu(dwconv_centred(qk_in.astype(f32), conv_w.astype(f32), conv_b.astype(f32)))
    q, k = jnp.split(qk, 2, axis=-1)
    q = q.reshape(Bsz, L, ML_HEADS, ML_DK).transpose(0, 2, 1, 3) * (ML_DK ** -0.5)
    k = k.reshape(Bsz, L, ML_HEADS, ML_DK).transpose(0, 2, 1, 3)
    v = v_in.astype(f32).reshape(Bsz, L, ML_HEADS, ML_DV).transpose(0, 2, 1, 3)
    gts = (gates_pre.astype(f32) + gate_b.astype(f32)).reshape(Bsz, L, 4, ML_HEADS).transpose(2, 0, 3, 1)
    li_f, lf_f = gts[0], jax.nn.log_sigmoid(gts[1])
    li_b, lf_b = gts[2], jax.nn.log_sigmoid(gts[3])
    qp, kp, vp = (_pad_front(t, META_PAD, 0.0) for t in (q, k, v))
    li_f, li_b = _pad_front(li_f, META_PAD, NEG), _pad_front(li_b, META_PAD, NEG)
    lf_f, lf_b = _pad_front(lf_f, META_PAD, 0.0), _pad_front(lf_b, META_PAD, 0.0)
    flip = lambda t: jnp.flip(t, axis=2)
    h_fwd = mlstm_chunkwise(qp, kp, vp, li_f, lf_f)
    h_bwd = flip(mlstm_chunkwise(flip(qp), flip(kp), flip(vp), flip(li_b), flip(lf_b)))
    h = (h_fwd + h_bwd)[:, :, META_PAD:, :]
    h = h * lax.rsqrt(jnp.mean(h * h, axis=-1, keepdims=True) + EPS) * head_g.astype(f32)[None, :, None, :]
    h = h.transpose(0, 2, 1, 3).reshape(Bsz, L, D_ML)
    return jax.nn.sigmoid(o_pre.astype(f32)) * h


def conv_ffn(x, w_gate, w_up, conv_w, conv_b, w_down):
    gt = dwconv_centred(x @ w_gate, conv_w, conv_b)
    return (jax.nn.silu(gt) * (x @ w_up)) @ w_down


def setup_inputs(seed: int = 0) -> dict:
    key = jax.random.key(seed)
    ks = jax.random.split(key, 32)
    f32 = jnp.float32
    nrm = lambda k, shape, s: jax.random.normal(k, shape, f32) * s
    G, P, H = S5_GROUPS, S5_STATE, S5_GROUP
    a_im0 = jnp.pi * jnp.arange(P, dtype=f32)
    gate_bias = jnp.concatenate([
        jnp.zeros((ML_HEADS,), f32), jnp.linspace(3.0, 6.0, ML_HEADS, dtype=f32),
        jnp.zeros((ML_HEADS,), f32), jnp.linspace(3.0, 6.0, ML_HEADS, dtype=f32)])
    return {
        'x': nrm(ks[0], (BATCH, SEQ, D_MODEL), 1.0),
        'meta': nrm(ks[1], (N_META, D_MODEL), 1.0),
        'norm_mix_g': 1.0 + nrm(ks[2], (DEPTH, D_MODEL), 0.01),
        'w_in': nrm(ks[3], (DEPTH, D_MODEL, D_IN), D_MODEL ** -0.5),
        's5_a_re': -0.5 + nrm(ks[4], (DEPTH, 2, G, P), 0.01),
        's5_a_im': a_im0 + nrm(ks[5], (DEPTH, 2, G, P), 0.01),
        's5_log_dt': jax.random.uniform(ks[6], (DEPTH, 2, G), f32, math.log(1e-3), math.log(1e-1)),
        's5_b_re': nrm(ks[7], (DEPTH, 2, G, P, H), (2 * H) ** -0.5),
        's5_b_im': nrm(ks[8], (DEPTH, 2, G, P, H), (2 * H) ** -0.5),
        's5_c_re': nrm(ks[9], (DEPTH, 2, G, H, P), P ** -0.5),
        's5_c_im': nrm(ks[10], (DEPTH, 2, G, H, P), P ** -0.5),
        's5_d': nrm(ks[11], (DEPTH, G, H), 1.0),
        's5_w_glu': nrm(ks[12], (DEPTH, D_S5, D_S5), D_S5 ** -0.5),
        's5_b_glu': nrm(ks[13], (DEPTH, D_S5), 0.01),
        's5_out_g': 1.0 + nrm(ks[14], (DEPTH, D_S5), 0.01),
        'ml_conv_w': nrm(ks[15], (DEPTH, QK_CONV, C_QK), QK_CONV ** -0.5),
        'ml_conv_b': nrm(ks[16], (DEPTH, C_QK), 0.01),
        'ml_gate_b': gate_bias + nrm(ks[17], (DEPTH, C_GATES), 0.1),
        'ml_head_g': 1.0 + nrm(ks[18], (DEPTH, ML_HEADS, ML_DV), 0.01),
        'w_out': nrm(ks[19], (DEPTH, D_MIX, D_MODEL), D_MIX ** -0.5),
        'norm_ffn_g': 1.0 + nrm(ks[20], (DEPTH, D_MODEL), 0.01),
        'ffn_w_gate': nrm(ks[21], (DEPTH, D_MODEL, D_FF), D_MODEL ** -0.5),
        'ffn_w_up': nrm(ks[22], (DEPTH, D_MODEL, D_FF), D_MODEL ** -0.5),
        'ffn_conv_w': nrm(ks[23], (DEPTH, FFN_CONV, D_FF), FFN_CONV ** -0.5),
        'ffn_conv_b': nrm(ks[24], (DEPTH, D_FF), 0.01),
        'ffn_w_down': nrm(ks[25], (DEPTH, D_FF, D_MODEL), D_FF ** -0.5),
        'final_g': 1.0 + nrm(ks[26], (D_MODEL,), 0.01),
    }


def reference(x, meta, norm_mix_g, w_in, s5_a_re, s5_a_im, s5_log_dt, s5_b_re, s5_b_im,
              s5_c_re, s5_c_im, s5_d, s5_w_glu, s5_b_glu, s5_out_g, ml_conv_w, ml_conv_b,
              ml_gate_b, ml_head_g, w_out, norm_ffn_g, ffn_w_gate, ffn_w_up, ffn_conv_w,
              ffn_conv_b, ffn_w_down, final_g):
    Bsz = x.shape[0]
    dt = x.dtype
    h = jnp.concatenate(
        [jnp.broadcast_to(meta.astype(dt), (Bsz, N_META, D_MODEL)), x], axis=1)
    splits = [D_S5, D_S5 + C_QK, D_S5 + C_QK + D_ML, D_S5 + C_QK + 2 * D_ML]
    for l in range(DEPTH):
        hn = rmsnorm(h, norm_mix_g[l])
        z = hn @ w_in[l]
        u_s5, qk_in, v_in, o_pre, gates_pre = jnp.split(z, splits, axis=-1)
        y_s5 = s5_mixer(u_s5, s5_a_re[l], s5_a_im[l], s5_log_dt[l], s5_b_re[l], s5_b_im[l],
                        s5_c_re[l], s5_c_im[l], s5_d[l], s5_w_glu[l], s5_b_glu[l])
        y_s5 = rmsnorm(y_s5, s5_out_g[l])
        y_ml = mlstm_mixer(qk_in, v_in, o_pre, gates_pre, ml_conv_w[l], ml_conv_b[l],
                           ml_gate_b[l], ml_head_g[l])
        y = jnp.concatenate([y_s5, y_ml], axis=-1).astype(dt)
        h = h + (y @ w_out[l]).astype(dt)
        hn = rmsnorm(h, norm_ffn_g[l])
        h = h + conv_ffn(hn, ffn_w_gate[l], ffn_w_up[l], ffn_conv_w[l], ffn_conv_b[l],
                         ffn_w_down[l]).astype(dt)
    return rmsnorm(h, final_g)[:, N_META:, :]
```

```python
import bisect
import math
import os
from contextlib import ExitStack

import numpy as np
import concourse.bass as bass
import concourse.mybir as mybir
from concourse.bass_utils import run_bass_kernel_spmd

F32 = mybir.dt.float32
BF16 = mybir.dt.bfloat16
I32 = mybir.dt.int32
ALU = mybir.AluOpType
AF = mybir.ActivationFunctionType
AX = mybir.AxisListType

L = 2064
NX = 2048
NM = 16
D = 2048
KT = 16
DIN = 4128
DFF = 5632
FT = 44
EPS = 1e-6
NCH = 258
TWO_PI = 2.0 * math.pi

PARAM_SHAPES = {
    'meta': (16, 2048), 'norm_mix_g': (4, 2048), 'w_in': (4, 2048, 4128),
    's5_a_re': (4, 2, 64, 64), 's5_a_im': (4, 2, 64, 64), 's5_log_dt': (4, 2, 64),
    's5_b_re': (4, 2, 64, 64, 16), 's5_b_im': (4, 2, 64, 64, 16),
    's5_c_re': (4, 2, 64, 16, 64), 's5_c_im': (4, 2, 64, 16, 64), 's5_d': (4, 64, 16),
    's5_w_glu': (4, 1024, 1024), 's5_b_glu': (4, 1024), 's5_out_g': (4, 1024),
    'ml_conv_w': (4, 3, 1024), 'ml_conv_b': (4, 1024), 'ml_gate_b': (4, 32),
    'ml_head_g': (4, 8, 128), 'w_out': (4, 2048, 2048), 'norm_ffn_g': (4, 2048),
    'ffn_w_gate': (4, 2048, 5632), 'ffn_w_up': (4, 2048, 5632), 'ffn_conv_w': (4, 3, 5632),
    'ffn_conv_b': (4, 5632), 'ffn_w_down': (4, 5632, 2048), 'final_g': (2048,),
}


class Buf:
    __slots__ = ("name", "last_w", "readers")

    def __init__(self, name="buf"):
        self.name = name
        self.last_w = None
        self.readers = []


class TR:
    NDMA = 48
    NSW = 12
    SAME_ENGINE_WAITS = True

    def __init__(self, nc, stack):
        self.nc = nc
        self.eng = {"pe": nc.tensor, "act": nc.scalar, "dve": nc.vector, "pool": nc.gpsimd, "sp": nc.sync}
        self.sem = {e: stack.enter_context(nc.semaphore("s_" + e)) for e in ("pe", "act", "dve", "pool")}
        self.ops = {e: [] for e in self.sem}
        self.marked_idx = {e: [] for e in self.sem}
        self.marked_val = {e: [] for e in self.sem}
        self.dsem = [stack.enter_context(nc.semaphore("d%d" % i)) for i in range(self.NDMA)]
        self.dval = [0] * self.NDMA
        self.dnext = {"hw": 0, "sw": 0}
        self.waited = {e: {} for e in self.eng}

    def _resolve(self, tok):
        if tok[0] == "d":
            return self.dsem[tok[1]], tok[2], ("d", tok[1])
        _, e, idx = tok
        rec = self.ops[e][idx]
        if rec[1] is None:
            mi = self.marked_idx[e]
            if not mi or mi[-1] < idx:
                rec[0].then_inc(self.sem[e], 1)
                val = len(mi) + 1
                rec[1] = val
                mi.append(idx)
                self.marked_val[e].append(val)
                return self.sem[e], val, ("c", e)
            j = bisect.bisect_left(mi, idx)
            return self.sem[e], self.marked_val[e][j], ("c", e)
        return self.sem[e], rec[1], ("c", e)

    def wait(self, e, tok):
        if tok is None:
            return
        if tok[0] == "c" and tok[1] == e:
            if e == "pe" or not self.SAME_ENGINE_WAITS:
                return
        sem, val, key = self._resolve(tok)
        if self.waited[e].get(key, 0) >= val:
            return
        self.eng[e].wait_ge(sem, val)
        self.waited[e][key] = val

    def _deps(self, e, R, W):
        for b in R:
            self.wait(e, b.last_w)
        for b in W:
            self.wait(e, b.last_w)
            for t in b.readers:
                self.wait(e, t)

    def _update(self, tok, R, W):
        for b in R:
            if tok[0] == "c":
                b.readers = [t for t in b.readers if not (t[0] == "c" and t[1] == tok[1])]
            b.readers.append(tok)
        for b in W:
            b.last_w = tok
            b.readers = []

    def op(self, e, fn, R=(), W=()):
        self._deps(e, R, W)
        ins = fn(self.eng[e])
        self.ops[e].append([ins, None])
        tok = ("c", e, len(self.ops[e]) - 1)
        self._update(tok, R, W)
        return tok

    def dma(self, e, out, in_, R=(), W=(), **kw):
        self._deps(e, R, W)
        if e == "pool":
            k = self.NDMA - self.NSW + self.dnext["sw"]
            self.dnext["sw"] = (self.dnext["sw"] + 1) % self.NSW
        else:
            k = self.dnext["hw"]
            self.dnext["hw"] = (self.dnext["hw"] + 1) % (self.NDMA - self.NSW)
        if self.dval[k] > 0:
            self.wait(e, ("d", k, self.dval[k]))
        ins = self.eng[e].dma_start(out=out, in_=in_, **kw)
        self.dval[k] += 16
        ins.then_inc(self.dsem[k], 16)
        tok = ("d", k, self.dval[k])
        self._update(tok, R, W)
        return tok

    def barrier(self):
        toks = []
        for e in self.sem:
            if self.ops[e]:
                toks.append(("c", e, len(self.ops[e]) - 1))
        for k in range(self.NDMA):
            if self.dval[k] > 0:
                toks.append(("d", k, self.dval[k]))
        for e in self.eng:
            for t in toks:
                if t[0] == "c" and t[1] == e:
                    continue
                self.wait(e, t)

    def finish(self):
        for k in range(self.NDMA):
            if self.dval[k] > 0:
                self.wait("sp", ("d", k, self.dval[k]))


class Ring:
    def __init__(self, items):
        self.items = items
        self.i = 0

    def next(self):
        it = self.items[self.i % len(self.items)]
        self.i += 1
        return it


def col_tiles(c0, c1, step):
    out = []
    c = c0
    while c < c1:
        out.append((c, min(c + step, c1)))
        c += step
    return out


class MK:
    def __init__(self, depth=4, phases="ABCDE", debug=(), ext_in=(), nl=4):
        self.depth = depth
        self._stages = {}
        self.phases = phases
        nc = bass.Bass("TRN2", target_bir_lowering=False)
        self.nc = nc
        self.st = ExitStack()
        self.st.enter_context(nc.allow_non_contiguous_dma(reason="small parameter / layout loads"))
        self.tr = TR(nc, self.st)
        self.x = nc.dram_tensor("x", [NX, D], F32, kind="ExternalInput").ap()
        self.P = {k: nc.dram_tensor(k, [(nl if (len(s) > 1 and s[0] == 4 and k != 'meta') else s[0])] + list(s[1:]), F32, kind="ExternalInput").ap() for k, s in PARAM_SHAPES.items()}
        self.out = nc.dram_tensor("out", [NX, D], F32, kind="ExternalOutput").ap()
        self.debug = debug
        self.ext_in = ext_in

        def scratch(name, shape, dt):
            if name in ext_in:
                kind = "ExternalInput"
            elif name in debug:
                kind = "ExternalOutput"
            else:
                kind = "Internal"
            return nc.dram_tensor(name, list(shape), dt, kind=kind).ap()

        self.hT = scratch("hT", [D, L], F32)
        self.hT2 = scratch("hT2", [D, L], F32)
        self.b_hT2 = [Buf("hT2_%d" % i) for i in range(KT)]
        self.U = scratch("U", [2, 8, 1024, NCH], BF16)
        self.QKT = scratch("QKT", [1024, L], BF16)
        self.V = scratch("V", [L, 1024], BF16)
        self.OS = scratch("OS", [L, 1024], BF16)
        self.G = scratch("G", [L, 32], F32)
        self.YT = scratch("YT", [D, L], BF16)
        self.HF = scratch("HF", [L, 1024], F32)
        self.b_HF = Buf("HF")
        self.Y5 = scratch("Y5", [8, 1024, NCH], BF16)
        self.b_Y5 = Buf("Y5")
        self.b_hT = [Buf("hT%d" % i) for i in range(KT)]
        self.b_U = [Buf("U%d" % i) for i in range(8)]
        self.b_QKT = [Buf("QKT%d" % i) for i in range(8)]
        self.b_V = Buf("V")
        self.b_OS = Buf("OS")
        self.b_G = Buf("G")
        self.b_YT = [Buf("YT%d" % i) for i in range(KT)]

        self.banks = []
        for i in range(8):
            t = self.st.enter_context(nc.psum_tensor("bank%d" % i, [128, 512], F32))
            self.banks.append((t, Buf("bank%d" % i)))
        self.bank_ring = Ring(self.banks)

        self.ident = self.sbuf(self.st, "ident", [128, 128], F32)
        self.identb = self.sbuf(self.st, "identb", [128, 128], BF16)
        self.onesb = self.sbuf(self.st, "onesb", [128, 128], BF16)
        self.b_const = Buf("const")
        tr = self.tr
        tr.op("pool", lambda e: e.memset(self.ident[:], 0.0), W=[self.b_const])
        tr.op("pool", lambda e: e.affine_select(out=self.ident[:], in_=self.ident[:], pattern=[[-1, 128]],
                                                compare_op=ALU.not_equal, fill=1.0, base=0, channel_multiplier=1),
              R=[self.b_const], W=[self.b_const])
        tr.op("pool", lambda e: e.tensor_copy(self.identb[:], self.ident[:]), R=[self.b_const], W=[self.b_const])
        tr.op("pool", lambda e: e.memset(self.onesb[:], 1.0), W=[self.b_const])

    def sbuf(self, st, name, shape, dt):
        self._uid = getattr(self, "_uid", 0) + 1
        return st.enter_context(self.nc.sbuf_tensor("%s_u%d" % (name, self._uid), list(shape), dt))

    def load_cols(self, st, name, vec_ap, n, buf):
        t = self.sbuf(st, name, [128, n], F32)
        self.tr.dma("sp", t[:], vec_ap.rearrange("(k p) -> p k", p=128), W=[buf])
        return t

    def phase_init(self):
        tr, nc = self.tr, self.nc
        with ExitStack() as st:
            xin = [(self.sbuf(st, "xin%d" % i, [128, D], F32), Buf()) for i in range(2)]
            stg = [(self.sbuf(st, "xstg%d" % i, [128, KT, 128], F32), Buf()) for i in range(2)]
            hTv = self.hT.rearrange("(k p) t -> p k t", p=128)
            for tt in range(17):
                xt, bx = xin[tt % 2]
                sg, bs = stg[tt % 2]
                if tt == 16:
                    rows, src, c0 = NM, self.P['meta'], 0
                else:
                    rows, src, c0 = 128, self.x[tt * 128:(tt + 1) * 128, :], NM + tt * 128
                tr.dma("sp", xt[0:rows, :], src, W=[bx])
                for q in range(4):
                    bank, bb = self.bank_ring.next()
                    for j in range(4):
                        k = q * 4 + j
                        tr.op("pe", lambda e, k=k, j=j: e.transpose(bank[:, j * 128:j * 128 + rows],
                                                                    xt[0:rows, k * 128:(k + 1) * 128],
                                                                    self.ident[0:rows, 0:rows]),
                              R=[bx, self.b_const], W=[bb])
                    eng = "act" if q % 2 == 0 else "dve"
                    src_ap = bank[:, :].rearrange("p (j t) -> p j t", j=4)[:, :, 0:rows]
                    if eng == "act":
                        tr.op("act", lambda e, q=q, s=src_ap: e.copy(sg[:, q * 4:(q + 1) * 4, 0:rows], s), R=[bb], W=[bs])
                    else:
                        tr.op("dve", lambda e, q=q, s=src_ap: e.tensor_copy(sg[:, q * 4:(q + 1) * 4, 0:rows], s), R=[bb], W=[bs])
                tr.dma("sp", hTv[:, :, c0:c0 + rows], sg[:, :, 0:rows], R=[bs], W=self.b_hT)
        tr.barrier()

    def norm_ws(self, st, nmax, tag):
        ws = {}
        ws['ring'] = Ring([(self.sbuf(st, "%s_nh%d" % (tag, i), [128, nmax], F32), Buf()) for i in range(4)])
        ws['sqr'] = Ring([(self.sbuf(st, "%s_nsq%d" % (tag, i), [128, nmax], BF16), Buf()) for i in range(2)])
        ws['rstd'] = self.sbuf(st, "%s_nrstd" % tag, [128, nmax], F32)
        ws['b_rstd'] = Buf()
        return ws

    def fm_norm(self, ws, hn, b_hn, gcols, b_g, c0, c1, off):
        tr = self.tr
        n = c1 - c0
        hTv = self.hT.rearrange("(k p) t -> p k t", p=128)
        ring, sqr, rstd, b_rstd = ws['ring'], ws['sqr'], ws['rstd'], ws['b_rstd']
        cts = col_tiles(0, n, 512)
        banks = [self.bank_ring.next() for _ in cts]
        for k in range(KT):
            hb, bh = ring.next()
            sq, bsq = sqr.next()
            tr.dma("sp", hb[:, 0:n], hTv[:, k, c0:c1], R=[self.b_hT[k]], W=[bh])
            tr.op("act", lambda e, sq=sq, hb=hb: e.activation(sq[:, 0:n], hb[:, 0:n], AF.Square), R=[bh], W=[bsq])
            for (a, b), (bank, bb) in zip(cts, banks):
                tr.op("pe", lambda e, a=a, b=b, bank=bank, sq=sq: e.matmul(bank[:, 0:b - a], self.onesb[:, :], sq[:, a:b],
                                                                          start=(k == 0), stop=(k == KT - 1)),
                      R=[bsq, self.b_const], W=[bb])
        for (a, b), (bank, bb) in zip(cts, banks):
            tr.op("act", lambda e, a=a, b=b, bank=bank: e.activation(rstd[:, a:b], bank[:, 0:b - a], AF.Sqrt,
                                                                    bias=self.epsc[:, 0:1], scale=1.0 / D),
                  R=[bb, self.b_const], W=[b_rstd])
        tr.op("dve", lambda e: e.reciprocal(rstd[:, 0:n], rstd[:, 0:n]), R=[b_rstd], W=[b_rstd])
        for k in range(KT):
            hb, bh = ring.next()
            tr.dma("sp", hb[:, 0:n], hTv[:, k, c0:c1], R=[self.b_hT[k]], W=[bh])
            tr.op("dve", lambda e, k=k, hb=hb: e.scalar_tensor_tensor(out=hn[:, k, off:off + n], in0=hb[:, 0:n],
                                                                      scalar=gcols[:, k:k + 1], in1=rstd[:, 0:n],
                                                                      op0=ALU.mult, op1=ALU.mult),
                  R=[bh, b_rstd, b_g], W=[b_hn])

    def wload(self, slot, b_slot, w_ap, kt, n0, n1):
        wv = w_ap.rearrange("(k p) n -> p k n", p=128)
        return self.tr.dma("pool", slot[:, 0:kt, 0:n1 - n0], wv[:, :, n0:n1], W=[b_slot])

    def phase_final(self):
        tr = self.tr
        with ExitStack() as st:
            b_g = Buf()
            gcols = self.load_cols(st, "fin_g", self.P['final_g'], KT, b_g)
            hn = self.sbuf(st, "fin_hn", [128, KT, L], F32)
            b_hn = Buf()
            ws = self.norm_ws(st, 1032, "F")
            for (c0, c1) in col_tiles(0, L, 1032):
                self.fm_norm(ws, hn, b_hn, gcols, b_g, c0, c1, c0)
            stg = Ring([(self.sbuf(st, "fin_stg%d" % i, [128, D], F32), Buf()) for i in range(2)])
            for tt in range(16):
                sg, bs = stg.next()
                c0 = NM + tt * 128
                for q in range(4):
                    bank, bb = self.bank_ring.next()
                    for j in range(4):
                        k = q * 4 + j
                        tr.op("pe", lambda e, k=k, j=j, bank=bank: e.transpose(bank[:, j * 128:(j + 1) * 128],
                                                                              hn[:, k, c0:c0 + 128], self.ident[:, :]),
                              R=[b_hn, self.b_const], W=[bb])
                    if q % 2 == 0:
                        tr.op("act", lambda e, q=q, bank=bank: e.copy(sg[:, q * 512:(q + 1) * 512], bank[:, :]), R=[bb], W=[bs])
                    else:
                        tr.op("dve", lambda e, q=q, bank=bank: e.tensor_copy(sg[:, q * 512:(q + 1) * 512], bank[:, :]), R=[bb], W=[bs])
                tr.dma("sp", self.out[tt * 128:(tt + 1) * 128, :], sg[:, :], R=[bs])

    def phase_A(self, l):
        tr = self.tr
        w_in = self.P['w_in'][l]
        with ExitStack() as st:
            b_g = Buf()
            gcols = self.load_cols(st, "A_g", self.P['norm_mix_g'][l], KT, b_g)
            hn = self.sbuf(st, "A_hn", [128, KT, L], BF16)
            b_hn = Buf()
            ws = self.norm_ws(st, 688, "A")
            for (c0, c1) in col_tiles(0, L, 688):
                self.fm_norm(ws, hn, b_hn, gcols, b_g, c0, c1, c0)
            self._stages = {}
            wr = Ring([(self.sbuf(st, "A_w%d" % i, [128, KT, 512], BF16), Buf()) for i in range(3)])
            cts = col_tiles(0, L, 344)
            ustg = Ring([(self.sbuf(st, "A_us%d" % i, [128, 8, NCH], BF16), Buf()) for i in range(4)])
            qstg = Ring([(self.sbuf(st, "A_qs%d" % i, [128, L], BF16), Buf()) for i in range(2)])
            blocks = [(n0, n0 + 512) for n0 in range(0, 4096, 512)] + [(4096, 4128)]
            if os.environ.get("A_BLOCKS"):
                lo_, hi_ = os.environ["A_BLOCKS"].split(":")
                blocks = blocks[int(lo_):int(hi_)]
            slots = {}
            if blocks:
                slots[0] = wr.next()
                self.wload(slots[0][0], slots[0][1], w_in, KT, *blocks[0])
            for bi, (n0, n1) in enumerate(blocks):
                if bi + 1 < len(blocks):
                    slots[bi + 1] = wr.next()
                    self.wload(slots[bi + 1][0], slots[bi + 1][1], w_in, KT, *blocks[bi + 1])
                wt, bw = slots[bi]
                if n0 < 2048:
                    for nt in range(4):
                        gi = (n0 // 128) + nt
                        if gi < 8:
                            uf, buf_f = ustg.next()
                            ur, buf_r = ustg.next()
                        else:
                            qs, bqs = qstg.next()
                        for (a, b) in cts:
                            bank, bb = self.bank_ring.next()
                            for k in range(KT):
                                tr.op("pe", lambda e, k=k, nt=nt, a=a, b=b, bank=bank: e.matmul(
                                    bank[:, 0:b - a], wt[:, k, nt * 128:(nt + 1) * 128], hn[:, k, a:b],
                                    start=(k == 0), stop=(k == KT - 1)), R=[bw, b_hn], W=[bb])
                            if gi < 8:
                                ca, cb = a // 8, b // 8
                                src = bank[:, 0:b - a].rearrange("p (c s) -> p s c", s=8)
                                tr.op("act", lambda e, src=src, ca=ca, cb=cb, uf=uf: e.copy(uf[:, :, ca:cb], src), R=[bb], W=[buf_f])
                                dst = ur[:, :, NCH - cb:NCH - ca]
                                if not os.environ.get("A_NOREV"):
                                    tr.op("act", lambda e, src=src, dst=dst: e.copy(dst, src[:, ::-1, ::-1]), R=[bb], W=[buf_r])
                            else:
                                if (a // 344) % 2 == 0:
                                    tr.op("act", lambda e, a=a, b=b, bank=bank, qs=qs: e.copy(qs[:, a:b], bank[:, 0:b - a]), R=[bb], W=[bqs])
                                else:
                                    tr.op("dve", lambda e, a=a, b=b, bank=bank, qs=qs: e.tensor_copy(qs[:, a:b], bank[:, 0:b - a]), R=[bb], W=[bqs])
                        if gi < 8:
                            dstf = self.U[0, :, gi * 128:(gi + 1) * 128, :].rearrange("s f c -> f s c")
                            dstr = self.U[1, :, gi * 128:(gi + 1) * 128, :].rearrange("s f c -> f s c")
                            if not os.environ.get("A_NOUDMA"):
                                tr.dma("sp", dstf, uf[:, :, :], R=[buf_f], W=[self.b_U[gi]])
                            if not os.environ.get("A_NOREVDMA"):
                                tr.dma("sp", dstr, ur[:, :, :], R=[buf_r], W=[self.b_U[gi]])
                        else:
                            tr.dma("sp", self.QKT[(gi - 8) * 128:(gi - 7) * 128, :], qs[:, :], R=[bqs], W=[self.b_QKT[gi - 8]])
                else:
                    nw = n1 - n0
                    for tt in range(0 if not os.environ.get("A_NOTM") else 17, 17 if not os.environ.get("A_NOMETA") else 16):
                        t0 = tt * 128
                        rows = min(128, L - t0)
                        bank, bb = self.bank_ring.next()
                        for k in range(KT):
                            tr.op("pe", lambda e, k=k, bank=bank, t0=t0, rows=rows, nw=nw: e.matmul(
                                bank[0:rows, 0:nw], hn[:, k, t0:t0 + rows], wt[:, k, 0:nw],
                                start=(k == 0), stop=(k == KT - 1)), R=[bw, b_hn], W=[bb])
                        if n0 < 3072:
                            sg, bs = self._tm_stage(st, "v")
                            if tt % 2 == 0:
                                tr.op("act", lambda e, bank=bank, sg=sg, rows=rows: e.copy(sg[0:rows, :], bank[0:rows, :]), R=[bb], W=[bs])
                            else:
                                tr.op("dve", lambda e, bank=bank, sg=sg, rows=rows: e.tensor_copy(sg[0:rows, :], bank[0:rows, :]), R=[bb], W=[bs])
                            tr.dma("sp", self.V[t0:t0 + rows, n0 - 2048:n1 - 2048], sg[0:rows, :], R=[bs], W=[self.b_V])
                        elif n0 < 4096:
                            sg, bs = self._tm_stage(st, "o")
                            tr.op("act", lambda e, bank=bank, sg=sg, rows=rows: e.activation(sg[0:rows, :], bank[0:rows, :], AF.Sigmoid), R=[bb], W=[bs])
                            tr.dma("sp", self.OS[t0:t0 + rows, n0 - 3072:n1 - 3072], sg[0:rows, :], R=[bs], W=[self.b_OS])
                        else:
                            sg, bs = self._tm_stage(st, "g")
                            tr.op("dve", lambda e, bank=bank, sg=sg, rows=rows: e.tensor_copy(sg[0:rows, :], bank[0:rows, 0:32]), R=[bb], W=[bs])
                            tr.dma("sp", self.G[t0:t0 + rows, :], sg[0:rows, :], R=[bs], W=[self.b_G])
        self._stages = {}
        tr.barrier()

    def _tm_stage(self, st, kind):
        if kind not in self._stages:
            if kind == "g":
                items = [(self.sbuf(st, "A_gs%d" % i, [128, 32], F32), Buf()) for i in range(2)]
            else:
                items = [(self.sbuf(st, "A_%ss%d" % (kind, i), [128, 512], BF16), Buf()) for i in range(3)]
            self._stages[kind] = Ring(items)
        return self._stages[kind].next()

    def phase_B(self, l):
        self.phase_B1(l)
        self.phase_B2(l)

    def phase_B1(self, l):
        tr, nc, P = self.tr, self.nc, self.P
        GB = 8
        with ExitStack() as st:
            sb = lambda name, shape, dt=F32: self.sbuf(st, "B_" + name, shape, dt)
            bS = Buf("B_setup")

            def dv(fn, R=(), W=()):
                return tr.op("dve", fn, R=R, W=W)

            def ac(fn, R=(), W=()):
                return tr.op("act", fn, R=R, W=W)

            cidx = sb("cidx", [128, NCH])
            mask8 = sb("mask8", [128, 8, 16])
            cre, cim = sb("cre", [128, 1024]), sb("cim", [128, 1024])
            dcol = sb("dcol", [128, 64])
            turn8, rho8 = sb("turn8", [128, 64]), sb("rho8", [128, 64])
            pwp_r, pwp_i, pwm_r, pwm_i = [sb(n, [128, 64, 8]) for n in ("pwp_r", "pwp_i", "pwm_r", "pwm_i")]
            bbr, bbi = sb("bbr", [128, 64, 16]), sb("bbi", [128, 64, 16])
            s_in = ExitStack()
            sb_outer = sb
            sb = lambda name, shape, dt=F32: self.sbuf(s_in, "Bt_" + name, shape, dt)
            ci = sb("ci", [128, NCH], I32)
            ki = sb("ki", [128, 8], I32)
            kvec = sb("kvec", [128, 8])
            tr.op("pool", lambda e: e.iota(ci[:, :], [[1, NCH]], base=0, channel_multiplier=0), W=[bS])
            tr.op("pool", lambda e: e.iota(ki[:, :], [[1, 8]], base=1, channel_multiplier=0), W=[bS])
            tr.op("pool", lambda e: e.memset(mask8[:, :, :], 1.0), W=[bS])
            tr.op("pool", lambda e: e.affine_select(out=mask8[:, :, :], in_=mask8[:, :, :], pattern=[[16, 8], [0, 16]],
                                                    compare_op=ALU.is_ge, fill=0.0, base=15, channel_multiplier=-1), R=[bS], W=[bS])
            dv(lambda e: e.tensor_copy(cidx[:, :], ci[:, :]), R=[bS], W=[bS])
            dv(lambda e: e.tensor_copy(kvec[:, :], ki[:, :]), R=[bS], W=[bS])
            are, aim, ldt = sb("are", [128, 64]), sb("aim", [128, 64]), sb("ldt", [128, 64])
            bre, bim = sb("bre", [128, 64, 16]), sb("bim", [128, 64, 16])
            xcr, xci = sb("xcr", [128, 8, 2, 64]), sb("xci", [128, 8, 2, 64])
            for d in range(2):
                ps = slice(d * 64, (d + 1) * 64)
                tr.dma("sp", are[ps, :], P['s5_a_re'][l, d].rearrange("g p -> p g"), W=[bS])
                tr.dma("sp", aim[ps, :], P['s5_a_im'][l, d].rearrange("g p -> p g"), W=[bS])
                tr.dma("sp", ldt[ps, :], P['s5_log_dt'][l, d].partition_broadcast(64), W=[bS])
                tr.dma("sp", bre[ps, :, :], P['s5_b_re'][l, d].rearrange("g p h -> p g h"), W=[bS])
                tr.dma("sp", bim[ps, :, :], P['s5_b_im'][l, d].rearrange("g p h -> p g h"), W=[bS])
                tr.dma("sp", xcr[:, :, d, :], P['s5_c_re'][l, d].rearrange("(gb g) h p -> (g h) gb p", g=8), W=[bS])
                tr.dma("sp", xci[:, :, d, :], P['s5_c_im'][l, d].rearrange("(gb g) h p -> (g h) gb p", g=8), W=[bS])
            for t in range(8):
                tr.dma("sp", dcol[t * 16:(t + 1) * 16, :], P['s5_d'][l].rearrange("g h -> h g"), W=[bS])
            for gb in range(8):
                for (xc, cc) in ((xcr, cre), (xci, cim)):
                    bank, bb = self.bank_ring.next()
                    tr.op("pe", lambda e, bank=bank, xc=xc, gb=gb: e.transpose(bank[:, 0:128], xc[:, gb, :, :].rearrange("q d p -> q (d p)"), self.ident[:, :]),
                          R=[bS, self.b_const], W=[bb])
                    ac(lambda e, bank=bank, cc=cc, gb=gb: e.copy(cc[:, gb * 128:(gb + 1) * 128], bank[:, 0:128]), R=[bb], W=[bS])
            S = lambda n: sb(n, [128, 64])
            dt_, xr, th, mag, t0_, t1_, t2_, lr, li, rden, pm1, zr, zi = [S(n) for n in
                ("dt", "xr", "th", "mag", "t0", "t1", "t2", "lr", "li", "rden", "pm1", "zr", "zi")]
            ti = sb("ti", [128, 64], I32)
            RW = dict(R=[bS], W=[bS])
            dv(lambda e: e.tensor_scalar_min(are[:, :], are[:, :], -1e-4), **RW)
            ac(lambda e: e.activation(dt_[:, :], ldt[:, :], AF.Exp), **RW)
            dv(lambda e: e.tensor_tensor(xr[:, :], are[:, :], dt_[:, :], ALU.mult), **RW)
            dv(lambda e: e.tensor_tensor(th[:, :], aim[:, :], dt_[:, :], ALU.mult), **RW)
            ac(lambda e: e.activation(mag[:, :], xr[:, :], AF.Exp), **RW)

            def sincos(out_s, out_c, u, ui, f_, shape_ap):
                ac(lambda e: e.copy(ui, u), **RW)
                dv(lambda e: e.tensor_tensor(f_, u, ui, ALU.subtract), **RW)
                ac(lambda e: e.activation(out_s, f_, AF.Sin, scale=TWO_PI), **RW)
                ac(lambda e: e.activation(ui, u, AF.Identity, bias=self.q25[:, 0:1], scale=1.0), **RW)
                dv(lambda e: e.scalar_tensor_tensor(out=f_, in0=u, scalar=0.25, in1=ui, op0=ALU.add, op1=ALU.subtract), **RW)
                ac(lambda e: e.activation(out_c, f_, AF.Sin, scale=TWO_PI), **RW)

            dv(lambda e: e.tensor_scalar_mul(t0_[:, :], th[:, :], 1.0 / TWO_PI), **RW)
            sincos(t1_[:, :], t2_[:, :], t0_[:, :], ti[:, :], rden[:, :], None)
            dv(lambda e: e.tensor_tensor(li[:, :], mag[:, :], t1_[:, :], ALU.mult), **RW)
            dv(lambda e: e.tensor_tensor(lr[:, :], mag[:, :], t2_[:, :], ALU.mult), **RW)
            dv(lambda e: e.tensor_tensor(t1_[:, :], are[:, :], are[:, :], ALU.mult), **RW)
            dv(lambda e: e.tensor_tensor(t2_[:, :], aim[:, :], aim[:, :], ALU.mult), **RW)
            dv(lambda e: e.tensor_tensor(t1_[:, :], t1_[:, :], t2_[:, :], ALU.add), **RW)
            dv(lambda e: e.reciprocal(rden[:, :], t1_[:, :]), **RW)
            dv(lambda e: e.tensor_scalar_add(pm1[:, :], lr[:, :], -1.0), **RW)
            dv(lambda e: e.tensor_tensor(t1_[:, :], pm1[:, :], are[:, :], ALU.mult), **RW)
            dv(lambda e: e.tensor_tensor(t2_[:, :], li[:, :], aim[:, :], ALU.mult), **RW)
            dv(lambda e: e.tensor_tensor(t1_[:, :], t1_[:, :], t2_[:, :], ALU.add), **RW)
            dv(lambda e: e.tensor_tensor(zr[:, :], t1_[:, :], rden[:, :], ALU.mult), **RW)
            dv(lambda e: e.tensor_tensor(t1_[:, :], li[:, :], are[:, :], ALU.mult), **RW)
            dv(lambda e: e.tensor_tensor(t2_[:, :], pm1[:, :], aim[:, :], ALU.mult), **RW)
            dv(lambda e: e.tensor_tensor(t1_[:, :], t1_[:, :], t2_[:, :], ALU.subtract), **RW)
            dv(lambda e: e.tensor_tensor(zi[:, :], t1_[:, :], rden[:, :], ALU.mult), **RW)
            dv(lambda e: e.tensor_scalar_mul(turn8[:, :], th[:, :], 8.0 / TWO_PI), **RW)
            ac(lambda e: e.activation(rho8[:, :], xr[:, :], AF.Exp, scale=8.0), **RW)
            T3 = lambda n, dt=F32: sb(n, [128, 64, 8], dt)
            uk, fk, sk, ck, xk, mp, mm = [T3(n) for n in ("uk", "fk", "sk", "ck", "xk", "mp", "mm")]
            uki = T3("uki", I32)
            kb = kvec[:, :].unsqueeze(1).to_broadcast([128, 64, 8])
            dv(lambda e: e.tensor_scalar_mul(t0_[:, :], th[:, :], 1.0 / TWO_PI), **RW)
            dv(lambda e: e.tensor_tensor(uk[:, :, :], t0_[:, :].unsqueeze(2).to_broadcast([128, 64, 8]), kb, ALU.mult), **RW)
            dv(lambda e: e.tensor_tensor(xk[:, :, :], xr[:, :].unsqueeze(2).to_broadcast([128, 64, 8]), kb, ALU.mult), **RW)
            sincos(sk[:, :, :], ck[:, :, :], uk[:, :, :], uki[:, :, :], fk[:, :, :], None)
            ac(lambda e: e.activation(mp[:, :, :], xk[:, :, :], AF.Exp), **RW)
            ac(lambda e: e.activation(mm[:, :, :], xk[:, :, :], AF.Exp, scale=-1.0), **RW)
            dv(lambda e: e.tensor_tensor(pwp_r[:, :, :], mp[:, :, :], ck[:, :, :], ALU.mult), **RW)
            dv(lambda e: e.tensor_tensor(pwp_i[:, :, :], mp[:, :, :], sk[:, :, :], ALU.mult), **RW)
            dv(lambda e: e.tensor_tensor(pwm_r[:, :, :], mm[:, :, :], ck[:, :, :], ALU.mult), **RW)
            dv(lambda e: e.scalar_tensor_tensor(out=pwm_i[:, :, :], in0=mm[:, :, :], scalar=-1.0, in1=sk[:, :, :], op0=ALU.mult, op1=ALU.mult), **RW)
            tb1 = sb("tb1", [128, 64, 16])
            zrb = zr[:, :].unsqueeze(2).to_broadcast([128, 64, 16])
            zib = zi[:, :].unsqueeze(2).to_broadcast([128, 64, 16])
            dv(lambda e: e.tensor_tensor(bbr[:, :, :], bre[:, :, :], zrb, ALU.mult), **RW)
            dv(lambda e: e.tensor_tensor(tb1[:, :, :], bim[:, :, :], zib, ALU.mult), **RW)
            dv(lambda e: e.tensor_tensor(bbr[:, :, :], bbr[:, :, :], tb1[:, :, :], ALU.subtract), **RW)
            dv(lambda e: e.tensor_tensor(bbi[:, :, :], bim[:, :, :], zrb, ALU.mult), **RW)
            dv(lambda e: e.tensor_tensor(tb1[:, :, :], bre[:, :, :], zib, ALU.mult), **RW)
            dv(lambda e: e.tensor_tensor(bbi[:, :, :], bbi[:, :, :], tb1[:, :, :], ALU.add), **RW)

            s_in.close()
            tr.barrier()
            sb = sb_outer
            M4 = lambda n, dt=F32: sb(n, [128, GB, 8, 16], dt)
            bm_r, bm_i, wc_r, wc_i, tm1, tm2 = [M4(n) for n in ("bm_r", "bm_i", "wc_r", "wc_i", "tm1", "tm2")]
            wci_r, wci_i, bmT_r, bmT_i, mt_f, mt_b = [M4(n, BF16) for n in ("wci_r", "wci_i", "bmT_r", "bmT_i", "mt_f", "mt_b")]
            bm_rb, bm_ib = M4("bm_rb", BF16), M4("bm_ib", BF16)
            b_mat, b_wci, b_bmT, b_mt = Buf(), Buf(), Buf(), Buf()
            T8 = lambda n, dt=F32: sb(n, [128, GB, NCH], dt)
            cosT, sinT, ut, ft_, vm_r, vm_i, fp_r, fp_i = [T8(n) for n in ("cosT", "sinT", "ut", "ft", "vm_r", "vm_i", "fp_r", "fp_i")]
            uti = T8("uti", I32)
            et_r, et_i, u_f, u_r, ystg = [T8(n, BF16) for n in ("et_r", "et_i", "u_f", "u_r", "ystg")]
            b_tab, b_vm, b_fp, b_et, b_uf, b_ur, b_ys = Buf(), Buf(), Buf(), Buf(), Buf(), Buf(), Buf()
            q1 = [(sb("q1_%d" % i, [128, NCH]), Buf()) for i in range(4)]
            q1r = Ring(q1)
            ybs = Ring([(sb("ybs%d" % i, [128, NCH]), Buf()) for i in range(2)])
            ysm = Ring([(sb("ysm%d" % i, [128, NCH]), Buf()) for i in range(2)])

            for blk in range(64 // GB):
                g0 = blk * GB
                gs = slice(g0, g0 + GB)
                for s in range(8):
                    tr.dma("sp", u_f[s * 16:(s + 1) * 16, :, :], self.U[0, s, g0 * 16:(g0 + GB) * 16, :].rearrange("(g h) c -> h g c", h=16),
                           R=self.b_U, W=[b_uf])
                    tr.dma("sp", u_r[s * 16:(s + 1) * 16, :, :], self.U[1, s, g0 * 16:(g0 + GB) * 16, :].rearrange("(g h) c -> h g c", h=16),
                           R=self.b_U, W=[b_ur])
                RM = dict(R=[bS, b_mat], W=[b_mat])
                pmr = pwm_r[:, gs, :].unsqueeze(3).to_broadcast([128, GB, 8, 16])
                pmi = pwm_i[:, gs, :].unsqueeze(3).to_broadcast([128, GB, 8, 16])
                ppr = pwp_r[:, gs, :].unsqueeze(3).to_broadcast([128, GB, 8, 16])
                ppi = pwp_i[:, gs, :].unsqueeze(3).to_broadcast([128, GB, 8, 16])
                bbr_b = bbr[:, gs, :].unsqueeze(2).to_broadcast([128, GB, 8, 16])
                bbi_b = bbi[:, gs, :].unsqueeze(2).to_broadcast([128, GB, 8, 16])
                cre_b = cre[:, g0 * 16:(g0 + GB) * 16].rearrange("q (g h) -> q g h", h=16).unsqueeze(2).to_broadcast([128, GB, 8, 16])
                cim_b = cim[:, g0 * 16:(g0 + GB) * 16].rearrange("q (g h) -> q g h", h=16).unsqueeze(2).to_broadcast([128, GB, 8, 16])
                A4 = lambda t: t[:, :, :, :]
                dv(lambda e: e.tensor_tensor(A4(bm_r), pmr, bbr_b, ALU.mult), **RM)
                dv(lambda e: e.tensor_tensor(A4(tm1), pmi, bbi_b, ALU.mult), **RM)
                dv(lambda e: e.tensor_tensor(A4(bm_r), A4(bm_r), A4(tm1), ALU.subtract), **RM)
                dv(lambda e: e.tensor_tensor(A4(bm_i), pmr, bbi_b, ALU.mult), **RM)
                dv(lambda e: e.tensor_tensor(A4(tm1), pmi, bbr_b, ALU.mult), **RM)
                dv(lambda e: e.tensor_tensor(A4(bm_i), A4(bm_i), A4(tm1), ALU.add), **RM)
                dv(lambda e: e.tensor_tensor(A4(wc_r), ppr, cre_b, ALU.mult), **RM)
                dv(lambda e: e.tensor_tensor(A4(tm1), ppi, cim_b, ALU.mult), **RM)
                dv(lambda e: e.tensor_tensor(A4(wc_r), A4(wc_r), A4(tm1), ALU.subtract), **RM)
                dv(lambda e: e.tensor_tensor(A4(wc_i), ppi, cre_b, ALU.mult), **RM)
                dv(lambda e: e.tensor_tensor(A4(tm1), ppr, cim_b, ALU.mult), **RM)
                dv(lambda e: e.tensor_tensor(A4(wc_i), A4(wc_i), A4(tm1), ALU.add), **RM)
                dv(lambda e: e.tensor_scalar_mul(A4(wc_i), A4(wc_i), -1.0), **RM)
                ac(lambda e: e.copy(A4(bm_rb), A4(bm_r)), **RM)
                ac(lambda e: e.copy(A4(bm_ib), A4(bm_i)), **RM)
                rb = rho8[:, gs].unsqueeze(2).unsqueeze(3).to_broadcast([128, GB, 8, 16])
                dv(lambda e: e.tensor_tensor(A4(tm1), A4(wc_r), rb, ALU.mult), **RM)
                dv(lambda e: e.tensor_tensor(A4(tm2), A4(wc_i), rb, ALU.mult), **RM)
                for (src, dst) in ((tm1, wci_r), (tm2, wci_i)):
                    ac(lambda e, src=src, dst=dst: e.copy(dst[0:64, :, :, :], src[0:64, :, :, :]), R=[b_mat], W=[b_wci])
                    ac(lambda e, src=src, dst=dst: e.copy(dst[64:128, :, :, :], src[64:128, :, ::-1, :]), R=[b_mat], W=[b_wci])
                for gl in range(GB):
                    g = g0 + gl
                    for (src, dst) in ((bm_rb, bmT_r), (bm_ib, bmT_i)):
                        bank, bb = self.bank_ring.next()
                        pst = bank[:, 0:64].bitcast(BF16)
                        tr.op("pe", lambda e, src=src, gl=gl, pst=pst: e.transpose(pst, src[:, gl, :, :].rearrange("q s h -> q (s h)"), self.identb[:, :]),
                              R=[b_mat, self.b_const], W=[bb])
                        ac(lambda e, dst=dst, gl=gl, pst=pst: e.copy(dst[:, gl, :, :].rearrange("q s h -> q (s h)"), pst), R=[bb], W=[b_bmT])
                    for d in range(2):
                        ps = slice(d * 64, (d + 1) * 64)
                        bank, bb = self.bank_ring.next()
                        tr.op("pe", lambda e, bank=bank, gl=gl, ps=ps: e.matmul(bank[:, 0:128], bm_r[ps, gl, :, :].rearrange("q s h -> q (s h)"),
                                                                                wc_r[ps, gl, :, :].rearrange("q s h -> q (s h)"), start=True, stop=False),
                              R=[b_mat], W=[bb])
                        tr.op("pe", lambda e, bank=bank, gl=gl, ps=ps: e.matmul(bank[:, 0:128], bm_i[ps, gl, :, :].rearrange("q s h -> q (s h)"),
                                                                                wc_i[ps, gl, :, :].rearrange("q s h -> q (s h)"), start=False, stop=True),
                              R=[b_mat], W=[bb])
                        q, bq = q1r.next()
                        dv(lambda e, q=q, bank=bank: e.tensor_tensor(q[:, 0:128], bank[:, 0:128], mask8[:, :, :].rearrange("q t h -> q (t h)"), ALU.mult),
                           R=[bb, bS], W=[bq])
                        if d == 0:
                            dv(lambda e, q=q, gl=gl, g=g: e.scalar_tensor_tensor(out=mt_f[:, gl, :, :].rearrange("q t h -> q (t h)"), in0=self.ident[:, :],
                                                                               scalar=dcol[:, g:g + 1], in1=q[:, 0:128], op0=ALU.mult, op1=ALU.add),
                               R=[bq, bS, self.b_const], W=[b_mt])
                        else:
                            ac(lambda e, q=q, gl=gl: e.copy(mt_b[:, gl, :, :], q[:, 0:128].rearrange("q (t h) -> q t h", h=16)[:, ::-1, :]), R=[bq], W=[b_mt])
                RT = dict(R=[bS, b_tab], W=[b_tab])
                A3 = lambda t: t[:, :, :]
                dv(lambda e: e.tensor_tensor(A3(ut), turn8[:, gs].unsqueeze(2).to_broadcast([128, GB, NCH]),
                                             cidx[:, :].unsqueeze(1).to_broadcast([128, GB, NCH]), ALU.mult), **RT)
                ac(lambda e: e.copy(A3(uti), A3(ut)), **RT)
                dv(lambda e: e.tensor_tensor(A3(ft_), A3(ut), A3(uti), ALU.subtract), **RT)
                ac(lambda e: e.activation(A3(sinT), A3(ft_), AF.Sin, scale=TWO_PI), **RT)
                ac(lambda e: e.activation(A3(uti), A3(ut), AF.Identity, bias=self.q25[:, 0:1], scale=1.0), **RT)
                dv(lambda e: e.scalar_tensor_tensor(out=A3(ft_), in0=A3(ut), scalar=0.25, in1=A3(uti), op0=ALU.add, op1=ALU.subtract), **RT)
                ac(lambda e: e.activation(A3(cosT), A3(ft_), AF.Sin, scale=TWO_PI), **RT)
                for gl in range(GB):
                    g = g0 + gl
                    vb = {}
                    for nm, bmT in (("r", bmT_r), ("i", bmT_i)):
                        bank, bb = self.bank_ring.next()
                        vb[nm] = (bank, bb)
                        lh = bmT[:, gl, :, :].rearrange("q s h -> q (s h)")
                        tr.op("pe", lambda e, bank=bank, lh=lh, gl=gl: e.matmul(bank[0:64, 0:NCH], lh[:, 0:64], u_f[:, gl, :], start=True, stop=True),
                              R=[b_bmT, b_uf], W=[bb])
                        tr.op("pe", lambda e, bank=bank, lh=lh, gl=gl: e.matmul(bank[64:128, 0:NCH], lh[:, 64:128], u_r[:, gl, :], start=True, stop=True),
                              R=[b_bmT, b_ur], W=[bb])
                    (vr, bvr), (vi, bvi) = vb["r"], vb["i"]
                    qa, bqa = q1r.next()
                    qb, bqb = q1r.next()
                    dv(lambda e, qa=qa, vr=vr, gl=gl: e.tensor_tensor(qa[:, :], vr[:, 0:NCH], cosT[:, gl, :], ALU.mult), R=[bvr, b_tab], W=[bqa])
                    dv(lambda e, qb=qb, vi=vi, gl=gl: e.tensor_tensor(qb[:, :], vi[:, 0:NCH], sinT[:, gl, :], ALU.mult), R=[bvi, b_tab], W=[bqb])
                    dv(lambda e, qa=qa, qb=qb, gl=gl: e.tensor_tensor(vm_r[:, gl, :], qa[:, :], qb[:, :], ALU.add), R=[bqa, bqb], W=[b_vm])
                    qa, bqa = q1r.next()
                    qb, bqb = q1r.next()
                    dv(lambda e, qa=qa, vi=vi, gl=gl: e.tensor_tensor(qa[:, :], vi[:, 0:NCH], cosT[:, gl, :], ALU.mult), R=[bvi, b_tab], W=[bqa])
                    dv(lambda e, qb=qb, vr=vr, gl=gl: e.tensor_tensor(qb[:, :], vr[:, 0:NCH], sinT[:, gl, :], ALU.mult), R=[bvr, b_tab], W=[bqb])
                    dv(lambda e, qa=qa, qb=qb, gl=gl: e.tensor_tensor(vm_i[:, gl, :], qa[:, :], qb[:, :], ALU.subtract), R=[bqa, bqb], W=[b_vm])
                    rhob = rho8[:, g:g + 1].to_broadcast([128, NCH])
                    dv(lambda e, gl=gl, rhob=rhob: e.tensor_tensor_scan(fp_r[:, gl, :], rhob, vm_r[:, gl, :], 0.0, ALU.mult, ALU.add), R=[b_vm, bS], W=[b_fp])
                    dv(lambda e, gl=gl, rhob=rhob: e.tensor_tensor_scan(fp_i[:, gl, :], rhob, vm_i[:, gl, :], 0.0, ALU.mult, ALU.add), R=[b_vm, bS], W=[b_fp])
                n1 = NCH - 1
                dv(lambda e: e.tensor_tensor(ut[:, :, 1:], cosT[:, :, 1:], fp_r[:, :, 0:n1], ALU.mult), R=[b_tab, b_fp], W=[b_tab])
                dv(lambda e: e.tensor_tensor(ft_[:, :, 1:], sinT[:, :, 1:], fp_i[:, :, 0:n1], ALU.mult), R=[b_tab, b_fp], W=[b_tab])
                tr.op("pool", lambda e: e.memset(et_r[:, :, 0:2], 0.0), W=[b_et])
                tr.op("pool", lambda e: e.memset(et_i[:, :, 0:2], 0.0), W=[b_et])
                dv(lambda e: e.tensor_tensor(vm_r[:, :, 1:], ut[:, :, 1:], ft_[:, :, 1:], ALU.subtract), R=[b_tab], W=[b_vm])
                ac(lambda e: e.copy(et_r[:, :, 1:], vm_r[:, :, 1:]), R=[b_vm], W=[b_et])
                dv(lambda e: e.tensor_tensor(ut[:, :, 1:], sinT[:, :, 1:], fp_r[:, :, 0:n1], ALU.mult), R=[b_tab, b_fp], W=[b_tab])
                dv(lambda e: e.tensor_tensor(ft_[:, :, 1:], cosT[:, :, 1:], fp_i[:, :, 0:n1], ALU.mult), R=[b_tab, b_fp], W=[b_tab])
                dv(lambda e: e.tensor_tensor(vm_i[:, :, 1:], ut[:, :, 1:], ft_[:, :, 1:], ALU.add), R=[b_tab], W=[b_vm])
                ac(lambda e: e.copy(et_i[:, :, 1:], vm_i[:, :, 1:]), R=[b_vm], W=[b_et])
                for gl in range(GB):
                    yb = {}
                    for d, (mt, uu, b_uu) in enumerate(((mt_f, u_f, b_uf), (mt_b, u_r, b_ur))):
                        ps = slice(d * 64, (d + 1) * 64)
                        bank, bb = self.bank_ring.next()
                        yb[d] = (bank, bb)
                        tr.op("pe", lambda e, bank=bank, mt=mt, uu=uu, gl=gl: e.matmul(bank[:, 0:NCH], mt[:, gl, :, :].rearrange("q t h -> q (t h)"), uu[:, gl, :],
                                                                                     start=True, stop=False), R=[b_mt, b_uu], W=[bb])
                        tr.op("pe", lambda e, bank=bank, gl=gl, ps=ps: e.matmul(bank[:, 0:NCH], wci_r[ps, gl, :, :].rearrange("q t h -> q (t h)"), et_r[ps, gl, :],
                                                                                start=False, stop=False), R=[b_wci, b_et], W=[bb])
                        tr.op("pe", lambda e, bank=bank, gl=gl, ps=ps: e.matmul(bank[:, 0:NCH], wci_i[ps, gl, :, :].rearrange("q t h -> q (t h)"), et_i[ps, gl, :],
                                                                                start=False, stop=True), R=[b_wci, b_et], W=[bb])
                    (yf, byf), (ybk, bybk) = yb[0], yb[1]
                    ybr, bybr = ybs.next()
                    ac(lambda e, ybr=ybr, ybk=ybk: e.copy(ybr[:, :], ybk[:, 0:NCH][:, ::-1]), R=[bybk], W=[bybr])
                    ys, bys = ysm.next()
                    dv(lambda e, ys=ys, yf=yf, ybr=ybr: e.tensor_tensor(ys[:, :], yf[:, 0:NCH], ybr[:, :], ALU.add), R=[byf, bybr], W=[bys])
                    ac(lambda e, ys=ys, gl=gl: e.activation(ystg[:, gl, :], ys[:, :], AF.Gelu), R=[bys], W=[b_ys])
                for t in range(8):
                    tr.dma("sp", self.Y5[t, g0 * 16:(g0 + GB) * 16, :].rearrange("(g h) c -> h g c", h=16), ystg[t * 16:(t + 1) * 16, :, :],
                           R=[b_ys], W=[self.b_Y5])
        tr.barrier()

    def phase_B2(self, l):
        tr, P = self.tr, self.P
        w = P['s5_w_glu'][l]
        with ExitStack() as st:
            sb = lambda name, shape, dt=F32: self.sbuf(st, "B2_" + name, shape, dt)
            b_g = Buf()
            gcols = self.load_cols(st, "B2_g", P['s5_out_g'][l], 8, b_g)
            bcols = self.load_cols(st, "B2_b", P['s5_b_glu'][l], 8, b_g)
            ygT = sb("ygT", [128, 8, L], BF16)
            b_yg = [Buf() for _ in range(8)]
            lr_ = Ring([(sb("ld%d" % i, [128, 8, NCH], BF16), Buf()) for i in range(2)])
            for ft in range(8):
                ld, bld = lr_.next()
                tr.dma("sp", ld[:, :, :], self.Y5[:, ft * 128:(ft + 1) * 128, :].rearrange("t f c -> f t c"), R=[self.b_Y5], W=[bld])
                eng = "act" if ft % 2 == 0 else "dve"
                dst = ygT[:, ft, :].rearrange("q (c t) -> q c t", t=8)
                src = ld[:, :, :].rearrange("q t c -> q c t")
                if eng == "act":
                    tr.op("act", lambda e, dst=dst, src=src: e.copy(dst, src), R=[bld], W=[b_yg[ft]])
                else:
                    tr.op("dve", lambda e, dst=dst, src=src: e.tensor_copy(dst, src), R=[bld], W=[b_yg[ft]])
            yglu = sb("yglu", [128, 8, L])
            b_ygl = [Buf() for _ in range(8)]
            wr = Ring([(sb("w%d" % i, [128, 8, 512], BF16), Buf()) for i in range(2)])
            sgr = Ring([(sb("sg%d" % i, [128, 344]), Buf()) for i in range(3)])
            cts = col_tiles(0, L, 344)
            slots = {0: wr.next()}
            self.wload(slots[0][0], slots[0][1], w, 8, 0, 512)
            for bi in range(2):
                if bi == 0:
                    slots[1] = wr.next()
                    self.wload(slots[1][0], slots[1][1], w, 8, 512, 1024)
                wt, bw = slots[bi]
                for nt in range(4):
                    gi = bi * 4 + nt
                    for (a, b) in cts:
                        bank, bb = self.bank_ring.next()
                        for k in range(8):
                            tr.op("pe", lambda e, k=k, nt=nt, a=a, b=b, bank=bank: e.matmul(
                                bank[:, 0:b - a], wt[:, k, nt * 128:(nt + 1) * 128], ygT[:, k, a:b],
                                start=(k == 0), stop=(k == 7)), R=[bw, b_yg[k]], W=[bb])
                        sg, bsg = sgr.next()
                        tr.op("act", lambda e, sg=sg, bank=bank, a=a, b=b, gi=gi: e.activation(sg[:, 0:b - a], bank[:, 0:b - a], AF.Sigmoid,
                                                                                           bias=bcols[:, gi:gi + 1], scale=1.0), R=[bb, b_g], W=[bsg])
                        tr.op("dve", lambda e, sg=sg, a=a, b=b, gi=gi: e.tensor_tensor(yglu[:, gi, a:b], ygT[:, gi, a:b], sg[:, 0:b - a], ALU.mult),
                              R=[bsg, b_yg[gi]], W=[b_ygl[gi]])
            sqr = Ring([(sb("sq%d" % i, [128, 344], BF16), Buf()) for i in range(3)])
            rstd = sb("rstd", [128, L])
            b_rstd = Buf()
            for (a, b) in cts:
                bank, bb = self.bank_ring.next()
                for k in range(8):
                    sq, bsq = sqr.next()
                    tr.op("act", lambda e, sq=sq, k=k, a=a, b=b: e.activation(sq[:, 0:b - a], yglu[:, k, a:b], AF.Square), R=[b_ygl[k]], W=[bsq])
                    tr.op("pe", lambda e, sq=sq, k=k, a=a, b=b, bank=bank: e.matmul(bank[:, 0:b - a], self.onesb[:, :], sq[:, 0:b - a],
                                                                                   start=(k == 0), stop=(k == 7)), R=[bsq, self.b_const], W=[bb])
                tr.op("act", lambda e, a=a, b=b, bank=bank: e.activation(rstd[:, a:b], bank[:, 0:b - a], AF.Sqrt, bias=self.epsc[:, 0:1], scale=1.0 / 1024),
                      R=[bb, self.b_const], W=[b_rstd])
            tr.op("dve", lambda e: e.reciprocal(rstd[:, :], rstd[:, :]), R=[b_rstd], W=[b_rstd])
            ost = Ring([(sb("ost%d" % i, [128, L], BF16), Buf()) for i in range(2)])
            for k in range(8):
                o, bo = ost.next()
                tr.op("dve", lambda e, o=o, k=k: e.scalar_tensor_tensor(out=o[:, :], in0=yglu[:, k, :], scalar=gcols[:, k:k + 1], in1=rstd[:, :],
                                                                      op0=ALU.mult, op1=ALU.mult), R=[b_ygl[k], b_rstd, b_g], W=[bo])
                tr.dma("sp", self.YT[k * 128:(k + 1) * 128, :], o[:, :], R=[bo], W=[self.b_YT[k]])
        tr.barrier()

    def phase_C(self, l):
        tr, P = self.tr, self.P
        NCK = 33
        with ExitStack() as st:
            sb = lambda name, shape, dt=F32: self.sbuf(st, "C_" + name, shape, dt)
            bK = Buf("C_const")

            def dv(fn, R=(), W=()):
                return tr.op("dve", fn, R=R, W=W)

            def ac(fn, R=(), W=()):
                return tr.op("act", fn, R=R, W=W)

            maskF, maskB = sb("maskF", [128, 64]), sb("maskB", [128, 64])
            triL, triU, blkA, blkB = sb("triL", [128, 128]), sb("triU", [128, 128]), sb("blkA", [128, 128]), sb("blkB", [128, 128])
            onec, ln8 = sb("onec", [128, 1]), sb("ln8", [128, 1])
            pl = lambda fn, R=(), W=(): tr.op("pool", fn, R=R, W=W)
            pl(lambda e: e.memset(onec[:, :], 1.0), W=[bK])
            pl(lambda e: e.memset(ln8[:, :], math.log(0.125)), W=[bK])
            for t_ in (maskF, maskB):
                pl(lambda e, t_=t_: e.memset(t_[:, :], 1.0), W=[bK])
            for t_ in (triL, triU, blkA, blkB):
                pl(lambda e, t_=t_: e.memset(t_[:, :], 0.0), W=[bK])
            pl(lambda e: e.memset(blkA[0:64, :], 1.0), R=[bK], W=[bK])
            pl(lambda e: e.memset(blkB[64:128, :], 1.0), R=[bK], W=[bK])
            for hf_ in range(2):
                ps = slice(hf_ * 64, hf_ * 64 + 64)
                pl(lambda e, ps=ps: e.affine_select(out=maskF[ps, :], in_=maskF[ps, :], pattern=[[1, 64]], compare_op=ALU.is_ge, fill=0.0,
                                                    base=0, channel_multiplier=-1), R=[bK], W=[bK])
                pl(lambda e, ps=ps: e.affine_select(out=maskB[ps, :], in_=maskB[ps, :], pattern=[[-1, 64]], compare_op=ALU.is_ge, fill=0.0,
                                                    base=0, channel_multiplier=1), R=[bK], W=[bK])
                pl(lambda e, ps=ps: e.memset(triL[ps, ps], 1.0), R=[bK], W=[bK])
                pl(lambda e, ps=ps: e.memset(triU[ps, ps], 1.0), R=[bK], W=[bK])
                pl(lambda e, ps=ps: e.affine_select(out=triL[ps, ps], in_=triL[ps, ps], pattern=[[1, 64]], compare_op=ALU.is_ge, fill=0.0,
                                                    base=0, channel_multiplier=-1), R=[bK], W=[bK])
                pl(lambda e, ps=ps: e.affine_select(out=triU[ps, ps], in_=triU[ps, ps], pattern=[[-1, 64]], compare_op=ALU.is_ge, fill=0.0,
                                                    base=0, channel_multiplier=1), R=[bK], W=[bK])
            cw = sb("cw", [128, 8, 3])
            cbias = sb("cbias", [128, 8])
            for j in range(3):
                tr.dma("sp", cw[:, :, j], P['ml_conv_w'][l][j].rearrange("(f p) -> p f", p=128), W=[bK])
            tr.dma("sp", cbias[:, :], P['ml_conv_b'][l].rearrange("(f p) -> p f", p=128), W=[bK])
            gbias = sb("gbias", [128, 32])
            tr.dma("sp", gbias[:, :], P['ml_gate_b'][l].partition_broadcast(128), W=[bK])
            headg = sb("headg", [128, 1024])
            tr.dma("sp", headg[:, :], P['ml_head_g'][l].rearrange("h v -> (h v)").partition_broadcast(128), W=[bK])

            qkT = sb("qkT", [128, 8, L], BF16)
            qA = sb("qA", [128, 4, L], BF16)
            qB = sb("qB", [128, 4, L], BF16)
            b_qk = [Buf() for _ in range(8)]
            for j in range(4):
                tr.op("pool", lambda e, j=j: e.memset(qA[:, j, :], 0.0), W=[b_qk[j]])
                tr.op("pool", lambda e, j=j: e.memset(qB[:, j, :], 0.0), W=[b_qk[j]])
            rawr = Ring([(sb("raw%d" % i, [128, L + 2]), Buf()) for i in range(2)])
            c1, c2 = sb("c1", [128, L]), sb("c2", [128, L])
            b_c1, b_c2 = Buf(), Buf()
            for (rw, brw) in rawr.items:
                dv(lambda e, rw=rw: e.memset(rw[:, :], 0.0), W=[brw])
            for ft in range(8):
                rw, brw = rawr.next()
                tr.dma("pool", rw[:, 1:L + 1], self.QKT[ft * 128:(ft + 1) * 128, :], R=[self.b_QKT[ft]], W=[brw])
                dv(lambda e, rw=rw, ft=ft: e.tensor_scalar(c1[:, :], rw[:, 0:L], cw[:, ft, 0:1], None, ALU.mult), R=[brw, bK], W=[b_c1])
                dv(lambda e, rw=rw, ft=ft: e.scalar_tensor_tensor(out=c2[:, :], in0=rw[:, 1:L + 1], scalar=cw[:, ft, 1:2], in1=c1[:, :],
                                                                  op0=ALU.mult, op1=ALU.add), R=[brw, bK, b_c1], W=[b_c2])
                dv(lambda e, rw=rw, ft=ft: e.scalar_tensor_tensor(out=c1[:, :], in0=rw[:, 2:L + 2], scalar=cw[:, ft, 2:3], in1=c2[:, :],
                                                                  op0=ALU.mult, op1=ALU.add), R=[brw, bK, b_c2], W=[b_c1])
                if ft >= 4:
                    ac(lambda e, ft=ft: e.activation(qkT[:, ft, :], c1[:, :], AF.Silu, bias=cbias[:, ft:ft + 1], scale=1.0), R=[b_c1, bK], W=[b_qk[ft]])
                else:
                    ac(lambda e, ft=ft: e.activation(qA[0:64, ft, :], c1[0:64, :], AF.Silu, bias=cbias[0:64, ft:ft + 1], scale=1.0), R=[b_c1, bK], W=[b_qk[ft]])
                    ac(lambda e, ft=ft: e.activation(qB[64:128, ft, :], c1[64:128, :], AF.Silu, bias=cbias[64:128, ft:ft + 1], scale=1.0), R=[b_c1, bK], W=[b_qk[ft]])
            cstop = int(os.environ.get("C_STOP", "99"))
            kTM = sb("kTM", [128, 17, 512], BF16)
            b_kTM = Buf()
            for tt in range(17 if cstop >= 2 else 0):
                rows = 128 if tt < 16 else 16
                bank, bb = self.bank_ring.next()
                pst = bank[:, :].bitcast(BF16)
                for j in range(4):
                    tr.op("pe", lambda e, j=j, tt=tt, rows=rows, pst=pst: e.transpose(pst[0:rows, j * 128:(j + 1) * 128],
                                                                                     qkT[:, 4 + j, tt * 128:tt * 128 + rows], self.identb[:, :]),
                          R=[b_qk[4 + j], self.b_const], W=[bb])
                ac(lambda e, tt=tt, rows=rows, pst=pst: e.copy(kTM[0:rows, tt, :], pst[0:rows, 0:512]), R=[bb], W=[b_kTM])
            alpha = sb("alpha", [128, 17, 16])
            beta = sb("beta", [128, 17, 16])
            egc = sb("egc", [128, 17, 2, 2, 4])
            b_gate = Buf()
            gts = sb("gts", [128, 32]); lfs = sb("lfs", [128, 16]); lis = sb("lis", [128, 16]); bcs = sb("bcs", [128, 16])
            egt = sb("egt", [128, 2, 16]); gt1 = sb("gt1", [128, 16])
            b_gt = Buf()
            RG = dict(R=[b_gt, bK], W=[b_gt])
            for tt in range(17 if cstop >= 3 else 0):
                rows = 128 if tt < 16 else 16
                if tt == 16:
                    dv(lambda e: e.memset(gts[:, :], 0.0), **RG)
                    dv(lambda e: e.memset(lfs[:, :], 0.0), **RG)
                    dv(lambda e: e.memset(lis[:, :], 0.0), **RG)
                rs = slice(0, rows)
                tr.dma("sp", gts[rs, :], self.G[tt * 128:tt * 128 + rows, :], R=[self.b_G, b_gt], W=[b_gt])
                dv(lambda e, rs=rs: e.tensor_tensor(gts[rs, :], gts[rs, :], gbias[rs, :], ALU.add), **RG)
                g4 = gts[rs, :].rearrange("q (a b) -> q a b", b=8)
                l2 = lfs[rs, :].rearrange("q (a b) -> q a b", b=8)
                i2 = lis[rs, :].rearrange("q (a b) -> q a b", b=8)
                ac(lambda e, l2=l2, g4=g4: e.activation(l2, g4[:, 1::2, :], AF.Exp, scale=-1.0), **RG)
                ac(lambda e, rs=rs: e.activation(lfs[rs, :], lfs[rs, :], AF.Ln, bias=onec[rs, 0:1], scale=1.0), **RG)
                dv(lambda e, rs=rs: e.tensor_scalar_mul(lfs[rs, :], lfs[rs, :], -1.0), **RG)
                dv(lambda e, i2=i2, g4=g4: e.tensor_copy(i2, g4[:, 0::2, :]), **RG)
                bank, bb = self.bank_ring.next()
                tr.op("pe", lambda e, bank=bank: e.matmul(bank[:, 0:8], triL[:, :], lfs[:, 0:8], start=True, stop=True), R=[b_gt, bK], W=[bb])
                tr.op("pe", lambda e, bank=bank: e.matmul(bank[:, 8:16], triU[:, :], lfs[:, 8:16], start=True, stop=True), R=[b_gt, bK], W=[bb])
                tr.op("pe", lambda e, bank=bank: e.matmul(bank[:, 16:32], blkA[:, :], lfs[:, :], start=True, stop=True), R=[b_gt, bK], W=[bb])
                tr.op("pe", lambda e, bank=bank: e.matmul(bank[:, 32:48], blkB[:, :], lfs[:, :], start=True, stop=True), R=[b_gt, bK], W=[bb])
                dv(lambda e, bank=bank: e.tensor_copy(bcs[:, :], bank[:, 0:16]), R=[bb, b_gt], W=[b_gt])
                ac(lambda e, tt=tt: e.activation(alpha[:, tt, :], bcs[:, :], AF.Exp, bias=ln8[:, 0:1], scale=1.0), R=[b_gt, bK], W=[b_gate])
                dv(lambda e: e.tensor_tensor(gt1[:, :], lis[:, :], bcs[:, :], ALU.subtract), **RG)
                ac(lambda e, tt=tt: e.activation(beta[:, tt, :], gt1[:, :], AF.Exp), R=[b_gt], W=[b_gate])
                ac(lambda e, bank=bank: e.activation(egt[:, :, :], bank[:, 16:48].rearrange("q (a b) -> q a b", b=16), AF.Exp), R=[bb, b_gt], W=[b_gt])
                e5 = egt[:, :, :].rearrange("q a (d s two) -> q a d s two", d=2, two=2)
                dv(lambda e, tt=tt, e5=e5: e.tensor_copy(egc[0:64, tt, :, :, :], e5[0:64, :, :, :, 0]), R=[b_gt], W=[b_gate])
                dv(lambda e, tt=tt, e5=e5: e.tensor_copy(egc[64:128, tt, :, :, :], e5[64:128, :, :, :, 1]), R=[b_gt], W=[b_gate])

            vr_ = Ring([(sb("vt%d" % i, [128, 1024], BF16), Buf()) for i in range(2)])
            vpr = Ring([(sb("vp%d" % i, [128, 8, 130], BF16), Buf()) for i in range(2)])
            pmr = Ring([(sb("pm%d" % i, [128, 8, 64], BF16), Buf()) for i in range(2)])
            hfr = Ring([(sb("hft%d" % i, [128, 1024]), Buf()) for i in range(2)])
            cst = sb("cst", [128, 4, 130])
            cstb = sb("cstb", [128, 4, 130], BF16)
            b_cst, b_cstb = Buf(), Buf()
            ctmp = Ring([(sb("ctmp%d" % i, [128, 130]), Buf()) for i in range(2)])
            dsm = Ring([(sb("dsm%d" % i, [128, 8]), Buf()) for i in range(4)])
            hsm = sb("hsm", [128, 1024]); hld = sb("hld", [128, 1024]); ost = sb("ost", [128, 1024], BF16)
            b_hsm, b_hld, b_ost = Buf(), Buf(), Buf()
            ss = sb("ss", [128, 8]); b_ss = Buf()
            sqj = sb("sqj", [128, 128]); b_sqj = Buf()
            ybf = sb("ybf", [128, 1024], BF16); b_ybf = Buf()
            ytst = Ring([(sb("ytst%d" % i, [128, 8, 128], BF16), Buf()) for i in range(2)])
            bank_groups = [(0, 3), (3, 6), (6, 8)]
            pmf = sb("pmf", [128, 8, 64]); b_pmf = Buf()

            for sweep in range(max(0, min(2, cstop - 3))):
                dv(lambda e: e.memset(cst[:, :, :], 0.0), R=[b_cst], W=[b_cst])
                ac(lambda e: e.copy(cstb[:, :, :], cst[:, :, :]), R=[b_cst], W=[b_cstb])
                order = list(range(NCK)) if sweep == 0 else list(range(NCK - 1, -1, -1))
                mask = maskF if sweep == 0 else maskB
                cur_tt = None
                for n in order:
                    if os.environ.get("C_NOLAST") and n == 32:
                        continue
                    tt, half = n // 2, n % 2
                    cs = 64 if n < 32 else 16
                    ps = slice(half * 64, half * 64 + cs)
                    tok0 = n * 64
                    if tt != cur_tt:
                        cur_tt = tt
                        rows = 128 if tt < 16 else 16
                        vt, bvt = vr_.next()
                        tr.dma("sp", vt[0:rows, :], self.V[tt * 128:tt * 128 + rows, :], R=[self.b_V], W=[bvt])
                        vp, bvp = vpr.next()
                        v3 = vt[:, :].rearrange("q (h v) -> q h v", v=128)
                        for h in range(8):
                            col = sweep * 8 + h
                            if h % 2 == 0 and not os.environ.get("C_VPACT"):
                                dv(lambda e, h=h, col=col, rows=rows, vp=vp, v3=v3, tt=tt: e.tensor_scalar(vp[0:rows, h, 0:128], v3[0:rows, h, :], beta[0:rows, tt, col:col + 1], None, ALU.mult),
                                   R=[bvt, b_gate], W=[bvp])
                            else:
                                ac(lambda e, h=h, col=col, rows=rows, vp=vp, v3=v3, tt=tt: e.activation(vp[0:rows, h, 0:128], v3[0:rows, h, :], AF.Copy, scale=beta[0:rows, tt, col:col + 1]),
                                   R=[bvt, b_gate], W=[bvp])
                        ac(lambda e, rows=rows, vp=vp, tt=tt: e.copy(vp[0:rows, :, 128], beta[0:rows, tt, sweep * 8:sweep * 8 + 8]), R=[b_gate], W=[bvp])
                        hft, bhft = hfr.next()
                        chunks_left = 2 if tt < 16 else 1
                    csw = int(os.environ.get("C_SW", "99"))
                    if csw < 2:
                        continue
                    bankP, bbP = self.bank_ring.next()
                    for h in range(8):
                        hp = slice((h % 2) * 64, (h % 2) * 64 + 64)
                        qz = qA if h % 2 == 0 else qB
                        tr.op("pe", lambda e, h=h, qz=qz, bankP=bankP: e.matmul(bankP[ps, h * 64:h * 64 + cs], qkT[:, 4 + h // 2, tok0:tok0 + cs],
                                                                             qz[:, h // 2, tok0:tok0 + cs], start=True, stop=True),
                              R=[b_qk[4 + h // 2], b_qk[h // 2]], W=[bbP])
                    pm, bpm = pmr.next()
                    if os.environ.get("C_NOMASK"):
                        continue
                    if os.environ.get("C_PMF32"):
                        dv(lambda e, bankP=bankP: e.tensor_tensor(pmf[ps, :, 0:cs], bankP[ps, :].rearrange("q (h t) -> q h t", t=64)[:, :, 0:cs],
                                                                  mask[ps, 0:cs].unsqueeze(1).to_broadcast([cs, 8, cs]), ALU.mult),
                           R=[bbP, bK], W=[b_pmf])
                        ac(lambda e, pm=pm: e.copy(pm[ps, :, 0:cs], pmf[ps, :, 0:cs]), R=[b_pmf], W=[bpm])
                    else:
                        dv(lambda e, pm=pm, bankP=bankP: e.tensor_tensor(pm[ps, :, 0:cs], bankP[ps, :].rearrange("q (h t) -> q h t", t=64)[:, :, 0:cs],
                                                                        mask[ps, 0:cs].unsqueeze(1).to_broadcast([cs, 8, cs]), ALU.mult),
                           R=[bbP, bK], W=[bpm])
                    if csw < 3:
                        continue
                    obanks = []
                    for (h0, h1) in bank_groups:
                        bankO, bbO = self.bank_ring.next()
                        obanks.append((bankO, bbO))
                        for h in range(h0, h1):
                            hp = slice((h % 2) * 64, (h % 2) * 64 + 64)
                            oc = (h - h0) * 129
                            tr.op("pe", lambda e, h=h, oc=oc, bankO=bankO, pm=pm, vp=vp: e.matmul(bankO[ps, oc:oc + 129], pm[ps, h, 0:cs], vp[ps, h, 0:129],
                                                                                            start=True, stop=False), R=[bpm, bvp], W=[bbO])
                            qz = qA if h % 2 == 0 else qB
                            tr.op("pe", lambda e, h=h, qz=qz, oc=oc, bankO=bankO: e.matmul(bankO[ps, oc:oc + 129], qz[:, h // 2, tok0:tok0 + cs], cstb[:, h // 2, 0:129],
                                                                                        start=False, stop=True), R=[b_qk[h // 2], b_cstb], W=[bbO])
                    if csw < 4:
                        continue
                    sbanks = []
                    for q2 in range(2):
                        bankS, bbS = self.bank_ring.next()
                        sbanks.append((bankS, bbS))
                        for h in range(q2 * 4, q2 * 4 + 4):
                            hp = slice((h % 2) * 64, (h % 2) * 64 + 64)
                            sc = ((h // 2) % 2) * 129
                            tr.op("pe", lambda e, h=h, hp=hp, sc=sc, bankS=bankS, vp=vp: e.matmul(bankS[hp, sc:sc + 129], kTM[ps, tt, h * 64:(h + 1) * 64], vp[ps, h, 0:129],
                                                                                            start=True, stop=True), R=[b_kTM, bvp], W=[bbS])
                    if csw < 5:
                        continue
                    for gi, (h0, h1) in enumerate(bank_groups):
                        bankO, bbO = obanks[gi]
                        nh = h1 - h0
                        a_ = alpha[ps, tt, sweep * 8 + h0:sweep * 8 + h1]
                        dm, bdm = dsm.next()
                        den = bankO[ps, 0:nh * 129].rearrange("q (h v) -> q h v", v=129)[:, :, 128]
                        dv(lambda e, dm=dm, den=den, a_=a_, nh=nh: e.tensor_tensor(dm[ps, 0:nh], den, a_, ALU.mult), R=[bbO, b_gate], W=[bdm])
                        dv(lambda e, dm=dm, nh=nh: e.scalar_tensor_tensor(out=dm[ps, 4:4 + nh], in0=dm[ps, 0:nh], scalar=-1.0, in1=dm[ps, 0:nh],
                                                                          op0=ALU.mult, op1=ALU.max), R=[bdm], W=[bdm])
                        dv(lambda e, dm=dm, nh=nh: e.tensor_scalar_max(dm[ps, 0:nh], dm[ps, 4:4 + nh], 1.0), R=[bdm], W=[bdm])
                        dv(lambda e, dm=dm, nh=nh: e.reciprocal(dm[ps, 0:nh], dm[ps, 0:nh]), R=[bdm], W=[bdm])
                        dv(lambda e, dm=dm, a_=a_, nh=nh: e.tensor_tensor(dm[ps, 0:nh], dm[ps, 0:nh], a_, ALU.mult), R=[bdm, b_gate], W=[bdm])
                        for h in range(h0, h1):
                            oc = (h - h0) * 129
                            ac(lambda e, h=h, oc=oc, bankO=bankO, dm=dm, hft=hft, h0=h0: e.activation(hft[ps, h * 128:(h + 1) * 128], bankO[ps, oc:oc + 128], AF.Copy,
                                                                                               scale=dm[ps, h - h0:h - h0 + 1]), R=[bbO, bdm], W=[bhft])
                    if csw < 6:
                        continue
                    for q2 in range(2):
                        bankS, bbS = sbanks[q2]
                        for sl in range(2):
                            slot = q2 * 2 + sl
                            sc = sl * 129
                            ct, bct = ctmp.next()
                            egs = egc[:, tt, half, sweep, slot:slot + 1]
                            dv(lambda e, ct=ct, slot=slot, egs=egs: e.tensor_scalar(ct[:, 0:129], cst[:, slot, 0:129], egs, None, ALU.mult), R=[b_cst, b_gate], W=[bct])
                            dv(lambda e, ct=ct, slot=slot, egs=egs, bankS=bankS, sc=sc: e.scalar_tensor_tensor(out=cst[:, slot, 0:129], in0=bankS[:, sc:sc + 129], scalar=egs,
                                                                                                  in1=ct[:, 0:129], op0=ALU.mult, op1=ALU.add),
                               R=[bbS, bct, b_gate], W=[b_cst])
                    ac(lambda e: e.copy(cstb[:, :, :], cst[:, :, :]), R=[b_cst], W=[b_cstb])
                    chunks_left -= 1
                    if chunks_left == 0:
                        rows = 128 if tt < 16 else 16
                        if sweep == 0:
                            tr.dma("sp", self.HF[tt * 128:tt * 128 + rows, :], hft[0:rows, :], R=[bhft], W=[self.b_HF])
                        else:
                            tr.dma("sp", hld[0:rows, :], self.HF[tt * 128:tt * 128 + rows, :], R=[self.b_HF], W=[b_hld])
                            tr.dma("sp", ost[0:rows, :], self.OS[tt * 128:tt * 128 + rows, :], R=[self.b_OS], W=[b_ost])
                            dv(lambda e, rows=rows, hft=hft: e.tensor_tensor(hsm[0:rows, :], hft[0:rows, :], hld[0:rows, :], ALU.add), R=[bhft, b_hld], W=[b_hsm])
                            for h in range(8):
                                ac(lambda e, h=h, rows=rows: e.activation(sqj[0:rows, :], hsm[0:rows, h * 128:(h + 1) * 128], AF.Square, accum_out=ss[0:rows, h:h + 1]),
                                   R=[b_hsm], W=[b_sqj, b_ss])
                            ac(lambda e, rows=rows: e.activation(ss[0:rows, :], ss[0:rows, :], AF.Sqrt, bias=self.epsc[0:rows, 0:1], scale=1.0 / 128), R=[b_ss, self.b_const], W=[b_ss])
                            dv(lambda e, rows=rows: e.reciprocal(ss[0:rows, :], ss[0:rows, :]), R=[b_ss], W=[b_ss])
                            for h in range(8):
                                ac(lambda e, h=h, rows=rows: e.activation(hsm[0:rows, h * 128:(h + 1) * 128], hsm[0:rows, h * 128:(h + 1) * 128], AF.Copy, scale=ss[0:rows, h:h + 1]),
                                   R=[b_ss], W=[b_hsm])
                            dv(lambda e, rows=rows: e.tensor_tensor(hsm[0:rows, :], hsm[0:rows, :], headg[0:rows, :], ALU.mult), R=[bK], W=[b_hsm])
                            dv(lambda e, rows=rows: e.tensor_tensor(ybf[0:rows, :], hsm[0:rows, :], ost[0:rows, :], ALU.mult), R=[b_hsm, b_ost], W=[b_ybf])
                            yts, byts = ytst.next()
                            for q2 in range(2):
                                bank, bb = self.bank_ring.next()
                                pst = bank[:, :].bitcast(BF16)
                                for j in range(4):
                                    h = q2 * 4 + j
                                    tr.op("pe", lambda e, h=h, j=j, rows=rows, pst=pst: e.transpose(pst[:, j * 128:j * 128 + rows], ybf[0:rows, h * 128:(h + 1) * 128],
                                                                                                 self.identb[0:rows, 0:rows]), R=[b_ybf, self.b_const], W=[bb])
                                src = pst[:, 0:512].rearrange("q (j t) -> q j t", t=128)[:, :, 0:rows]
                                ac(lambda e, src=src, yts=yts, rows=rows, q2=q2: e.copy(yts[:, q2 * 4:q2 * 4 + 4, 0:rows], src), R=[bb], W=[byts])
                            tr.dma("sp", self.YT[1024:2048, tt * 128:tt * 128 + rows].rearrange("(h v) t -> v h t", v=128), yts[:, :, 0:rows],
                                   R=[byts], W=self.b_YT[8:16])
        tr.barrier()

    def phase_D(self, l):
        tr = self.tr
        w = self.P['w_out'][l]
        hTv = self.hT.rearrange("(k p) t -> p k t", p=128)
        with ExitStack() as st:
            yt = self.sbuf(st, "D_y", [128, KT, L], BF16)
            b_y = Buf()
            for k in range(KT):
                tr.dma("sp", yt[:, k, :], self.YT[k * 128:(k + 1) * 128, :], R=[self.b_YT[k]], W=[b_y])
            wr = Ring([(self.sbuf(st, "D_w%d" % i, [128, KT, 512], BF16), Buf()) for i in range(3)])
            hr = Ring([(self.sbuf(st, "D_h%d" % i, [128, L], F32), Buf()) for i in range(2)])
            cts = col_tiles(0, L, 344)
            blocks = [(n0, n0 + 512) for n0 in range(0, D, 512)]
            slots = {0: wr.next()}
            self.wload(slots[0][0], slots[0][1], w, KT, *blocks[0])
            for bi, (n0, n1) in enumerate(blocks):
                if bi + 1 < len(blocks):
                    slots[bi + 1] = wr.next()
                    self.wload(slots[bi + 1][0], slots[bi + 1][1], w, KT, *blocks[bi + 1])
                wt, bw = slots[bi]
                for nt in range(4):
                    gi = n0 // 128 + nt
                    hb, bh = hr.next()
                    tr.dma("sp", hb[:, :], hTv[:, gi, :], R=[self.b_hT[gi]], W=[bh])
                    for (a, b) in cts:
                        bank, bb = self.bank_ring.next()
                        for k in range(KT):
                            tr.op("pe", lambda e, k=k, nt=nt, a=a, b=b, bank=bank: e.matmul(
                                bank[:, 0:b - a], wt[:, k, nt * 128:(nt + 1) * 128], yt[:, k, a:b],
                                start=(k == 0), stop=(k == KT - 1)), R=[bw, b_y], W=[bb])
                        tr.op("dve", lambda e, a=a, b=b, bank=bank, hb=hb: e.tensor_tensor(hb[:, a:b], bank[:, 0:b - a], hb[:, a:b], ALU.add),
                              R=[bb], W=[bh])
                    tr.dma("sp", hTv[:, gi, :], hb[:, :], R=[bh], W=[self.b_hT[gi]])
        tr.barrier()

    def phase_E(self, l):
        tr = self.tr
        wg, wu, wd = self.P['ffn_w_gate'][l], self.P['ffn_w_up'][l], self.P['ffn_w_down'][l]
        hTv = self.hT.rearrange("(k p) t -> p k t", p=128)
        hTo = self.hT2.rearrange("(k p) t -> p k t", p=128)
        PT = 688
        with ExitStack() as st:
            b_g = Buf()
            gcols = self.load_cols(st, "E_g", self.P['norm_ffn_g'][l], KT, b_g)
            cw = self.sbuf(st, "E_cw", [128, FT, 3], F32)
            cb = self.sbuf(st, "E_cb", [128, FT], F32)
            b_cw = Buf()
            for j in range(3):
                tr.dma("sp", cw[:, :, j], self.P['ffn_conv_w'][l][j].rearrange("(f p) -> p f", p=128), W=[b_cw])
            tr.dma("sp", cb[:, :], self.P['ffn_conv_b'][l].rearrange("(f p) -> p f", p=128), W=[b_cw])
            hn = self.sbuf(st, "E_hn", [128, KT, PT + 2], BF16)
            b_hn = Buf()
            act = self.sbuf(st, "E_act", [128, FT, PT], BF16)
            b_act = [Buf() for _ in range(FT)]
            wgr = Ring([(self.sbuf(st, "E_wg%d" % i, [128, KT, 256], BF16), Buf()) for i in range(2)])
            wur = Ring([(self.sbuf(st, "E_wu%d" % i, [128, KT, 256], BF16), Buf()) for i in range(2)])
            wdr = Ring([(self.sbuf(st, "E_wd%d" % i, [128, FT, 128], BF16), Buf()) for i in range(2)])
            hr = Ring([(self.sbuf(st, "E_h%d" % i, [128, PT], F32), Buf()) for i in range(2)])
            t1r = Ring([(self.sbuf(st, "E_t1%d" % i, [128, 344], F32), Buf()) for i in range(2)])
            t2r = Ring([(self.sbuf(st, "E_t2%d" % i, [128, 344], F32), Buf()) for i in range(2)])
            t3r = Ring([(self.sbuf(st, "E_t3%d" % i, [128, 344], F32), Buf()) for i in range(2)])
            ws = self.norm_ws(st, PT + 2, "E")
            for p in range(3):
                t0 = p * PT
                lo = max(t0 - 1, 0)
                hi = min(t0 + PT + 1, L)
                tr.op("pool", lambda e: e.memset(hn[:, :, :], 0.0), W=[b_hn])
                self.fm_norm(ws, hn, b_hn, gcols, b_g, lo, hi, lo - (t0 - 1))
                nblk = DFF // 256
                sg = {0: wgr.next()}
                su = {0: wur.next()}
                self.wload(sg[0][0], sg[0][1], wg, KT, 0, 256)
                self.wload(su[0][0], su[0][1], wu, KT, 0, 256)
                for bi in range(nblk):
                    if bi + 1 < nblk:
                        sg[bi + 1] = wgr.next()
                        su[bi + 1] = wur.next()
                        self.wload(sg[bi + 1][0], sg[bi + 1][1], wg, KT, (bi + 1) * 256, (bi + 2) * 256)
                        self.wload(su[bi + 1][0], su[bi + 1][1], wu, KT, (bi + 1) * 256, (bi + 2) * 256)
                    wgt, bwg = sg[bi]
                    wut, bwu = su[bi]
                    for nt in range(2):
                        f = bi * 2 + nt
                        for ct in range(2):
                            e0 = ct * 344
                            gb, bgb = self.bank_ring.next()
                            ub, bub = self.bank_ring.next()
                            for k in range(KT):
                                tr.op("pe", lambda e, k=k, nt=nt, e0=e0, gb=gb: e.matmul(
                                    gb[:, 0:346], wgt[:, k, nt * 128:(nt + 1) * 128], hn[:, k, e0:e0 + 346],
                                    start=(k == 0), stop=(k == KT - 1)), R=[bwg, b_hn], W=[bgb])
                            for k in range(KT):
                                tr.op("pe", lambda e, k=k, nt=nt, e0=e0, ub=ub: e.matmul(
                                    ub[:, 0:344], wut[:, k, nt * 128:(nt + 1) * 128], hn[:, k, e0 + 1:e0 + 345],
                                    start=(k == 0), stop=(k == KT - 1)), R=[bwu, b_hn], W=[bub])
                            t1, bt1 = t1r.next()
                            t2, bt2 = t2r.next()
                            t3, bt3 = t3r.next()
                            tr.op("dve", lambda e, f=f, gb=gb, t1=t1: e.tensor_scalar(t1[:, :], gb[:, 0:344], cw[:, f, 0:1], None, ALU.mult),
                                  R=[bgb, b_cw], W=[bt1])
                            tr.op("dve", lambda e, f=f, gb=gb, t1=t1, t2=t2: e.scalar_tensor_tensor(
                                out=t2[:, :], in0=gb[:, 1:345], scalar=cw[:, f, 1:2], in1=t1[:, :], op0=ALU.mult, op1=ALU.add),
                                  R=[bgb, b_cw, bt1], W=[bt2])
                            tr.op("dve", lambda e, f=f, gb=gb, t1=t1, t2=t2: e.scalar_tensor_tensor(
                                out=t1[:, :], in0=gb[:, 2:346], scalar=cw[:, f, 2:3], in1=t2[:, :], op0=ALU.mult, op1=ALU.add),
                                  R=[bgb, b_cw, bt2], W=[bt1])
                            tr.op("act", lambda e, f=f, t1=t1, t3=t3: e.activation(t3[:, :], t1[:, :], AF.Silu, bias=cb[:, f:f + 1], scale=1.0),
                                  R=[bt1, b_cw], W=[bt3])
                            tr.op("dve", lambda e, f=f, ub=ub, t3=t3, ct=ct: e.tensor_tensor(
                                act[:, f, ct * 344:(ct + 1) * 344], ub[:, 0:344], t3[:, :], ALU.mult),
                                  R=[bub, bt3], W=[b_act[f]])
                sd = {0: wdr.next()}
                self.wload(sd[0][0], sd[0][1], wd, FT, 0, 128)
                for i in range(KT):
                    if i + 1 < KT:
                        sd[i + 1] = wdr.next()
                        self.wload(sd[i + 1][0], sd[i + 1][1], wd, FT, (i + 1) * 128, (i + 2) * 128)
                    wdt, bwd = sd[i]
                    hb, bh = hr.next()
                    tr.dma("sp", hb[:, :], hTv[:, i, t0:t0 + PT], R=[self.b_hT[i]], W=[bh])
                    for ct in range(2):
                        bank, bb = self.bank_ring.next()
                        for f in range(FT):
                            tr.op("pe", lambda e, f=f, ct=ct, bank=bank: e.matmul(
                                bank[:, 0:344], wdt[:, f, 0:128], act[:, f, ct * 344:(ct + 1) * 344],
                                start=(f == 0), stop=(f == FT - 1)), R=[bwd, b_act[f]], W=[bb])
                        tr.op("dve", lambda e, ct=ct, bank=bank, hb=hb: e.tensor_tensor(
                            hb[:, ct * 344:(ct + 1) * 344], bank[:, 0:344], hb[:, ct * 344:(ct + 1) * 344], ALU.add),
                              R=[bb], W=[bh])
                    tr.dma("sp", hTo[:, i, t0:t0 + PT], hb[:, :], R=[bh], W=[self.b_hT2[i]])
        tr.barrier()
        self.hT, self.hT2 = self.hT2, self.hT
        self.b_hT, self.b_hT2 = self.b_hT2, self.b_hT

    def build(self):
        tr = self.tr
        self.epsc = self.sbuf(self.st, "epsc", [128, 1], F32)
        tr.op("pool", lambda e: e.memset(self.epsc[:], EPS), W=[self.b_const])
        self.q25 = self.sbuf(self.st, "q25", [128, 1], F32)
        tr.op("pool", lambda e: e.memset(self.q25[:], 0.25), W=[self.b_const])
        if "hT" not in self.ext_in:
            self.phase_init()
        for l in range(self.depth):
            if "A" in self.phases:
                self.phase_A(l)
            if "B" in self.phases:
                self.phase_B(l)
            if "C" in self.phases:
                self.phase_C(l)
            if "D" in self.phases:
                self.phase_D(l)
            if "E" in self.phases:
                self.phase_E(l)
        if "F" in self.phases:
            self.phase_final()
        tr.barrier()
        tr.finish()
        self.st.close()
        return self.nc


def kernel(**inputs):
    mk = MK(depth=4, phases="ABCDEF")
    nc = mk.build()
    x = np.ascontiguousarray(inputs['x'], dtype=np.float32)
    params = {k: np.ascontiguousarray(inputs[k], dtype=np.float32) for k in PARAM_SHAPES}
    in_maps = []
    for c in range(8):
        m = dict(params)
        m['x'] = x[c]
        in_maps.append(m)
    res = run_bass_kernel_spmd(nc, in_maps, core_ids=list(range(8)))
    return np.stack([np.asarray(r['out'], dtype=np.float32) for r in res.results], axis=0)
```
